# Optimizing a Trainium2 kernel written in Bass

```python
import jax
import jax.numpy as jnp
from jax import lax
import numpy as np

D_MODEL = 1024
BATCH = 32
SEQ = 2048
DEPTH = 1
DEC_BATCH = 1
DEC_SEQ = 16384
PAST_LEN = 128

N_META = 16
H_A = 8
Q_LORA = 256
KV_LORA = 128
NOPE = 64
ROPE = 32
QK_DIM = NOPE + ROPE
V_DIM = 64
ROPE_THETA = 10000.0
Q_BLOCK = 128
H_B = 4
DH_B = 128
CHUNK = 64
NEG = -1e30
D_FF = 2816
CONV_W = 3
EPS = 1e-6
IN_SIZES = (Q_LORA, KV_LORA, ROPE, H_B * DH_B, H_B * DH_B, H_B * DH_B, H_B * DH_B, 2 * H_B, 2 * H_B, D_MODEL, D_MODEL)
IN_COLS = sum(IN_SIZES)

kernel_name = "hybrid_mla_mlstm_encoder"


def _rmsnorm(x, g):
    xf = x.astype(jnp.float32)
    y = xf * lax.rsqrt(jnp.mean(xf * xf, axis=-1, keepdims=True) + EPS)
    return (y * g.astype(jnp.float32)).astype(x.dtype)


def _rope_tables(L):
    inv = ROPE_THETA ** (-jnp.arange(0, ROPE, 2, dtype=jnp.float32) / ROPE)
    ang = jnp.arange(L, dtype=jnp.float32)[:, None] * inv[None, :]
    return jnp.cos(ang), jnp.sin(ang)


def _rope(x, cos, sin):
    x1, x2 = jnp.split(x.astype(jnp.float32), 2, axis=-1)
    c = cos[:, None, :]
    s = sin[:, None, :]
    return jnp.concatenate([x1 * c - x2 * s, x1 * s + x2 * c], axis=-1).astype(x.dtype)


def _block_attention(q, k, v):
    B, H, L, dq = q.shape
    nb = -(-L // Q_BLOCK)
    Lq = nb * Q_BLOCK
    qp = jnp.pad(q, ((0, 0), (0, 0), (0, Lq - L), (0, 0)))
    qb = jnp.moveaxis(qp.reshape(B, H, nb, Q_BLOCK, dq), 2, 0)
    scale = QK_DIM ** -0.5

    def block(qi):
        s = jnp.einsum('bhqd,bhkd->bhqk', qi, k).astype(jnp.float32) * scale
        p = jax.nn.softmax(s, axis=-1)
        return jnp.einsum('bhqk,bhkd->bhqd', p.astype(v.dtype), v)

    o = lax.map(block, qb)
    return jnp.moveaxis(o, 0, 2).reshape(B, H, Lq, v.shape[-1])[:, :, :L]


def _mlstm_dir(q, k, v, i_pre, log_f):
    B, H, T, dk = q.shape
    dv = v.shape[-1]
    N = T // CHUNK
    f32 = jnp.float32
    qc = q.reshape(B, H, N, CHUNK, dk).astype(f32)
    kc = k.reshape(B, H, N, CHUNK, dk).astype(f32) * (dk ** -0.5)
    vc = v.reshape(B, H, N, CHUNK, dv).astype(f32)
    ic = i_pre.reshape(B, H, N, CHUNK)
    bc = jnp.cumsum(log_f.reshape(B, H, N, CHUNK), axis=-1)
    gc = bc[..., -1]
    a = gc[..., None] - bc + ic
    m_loc = jnp.max(a, axis=-1)
    w = jnp.exp(a - m_loc[..., None])
    dC = jnp.einsum('bhncd,bhnce->bhnde', kc * w[..., None], vc)
    dn = jnp.einsum('bhncd,bhnc->bhnd', kc, w)

    def step(carry, xs):
        C, n, m = carry
        dC_, dn_, g_, ml_ = xs
        m_new = jnp.maximum(g_ + m, ml_)
        sp = jnp.exp(g_ + m - m_new)
        sl = jnp.exp(ml_ - m_new)
        C_new = sp[..., None, None] * C + sl[..., None, None] * dC_
        n_new = sp[..., None] * n + sl[..., None] * dn_
        return (C_new, n_new, m_new), (C, n, m)

    init = (jnp.zeros((B, H, dk, dv), f32), jnp.zeros((B, H, dk), f32), jnp.full((B, H), NEG, f32))
    xs = (jnp.moveaxis(dC, 2, 0), jnp.moveaxis(dn, 2, 0), jnp.moveaxis(gc, 2, 0), jnp.moveaxis(m_loc, 2, 0))
    _, (Cs, ns, ms) = lax.scan(step, init, xs)
    Cs = jnp.moveaxis(Cs, 0, 2)
    ns = jnp.moveaxis(ns, 0, 2)
    ms = jnp.moveaxis(ms, 0, 2)

    D = bc[..., :, None] - bc[..., None, :] + ic[..., None, :]
    tril = jnp.tril(jnp.ones((CHUNK, CHUNK), dtype=bool))
    D = jnp.where(tril, D, -jnp.inf)
    inter = bc + ms[..., None]
    m_t = jnp.maximum(jnp.max(D, axis=-1), inter)
    P = jnp.exp(D - m_t[..., None])
    s = jnp.einsum('bhntd,bhnsd->bhnts', qc, kc) * P
    e_in = jnp.exp(inter - m_t)
    num = jnp.einsum('bhnts,bhnse->bhnte', s, vc) + e_in[..., None] * jnp.einsum('bhntd,bhnde->bhnte', qc, Cs)
    den = jnp.sum(s, axis=-1) + e_in * jnp.einsum('bhntd,bhnd->bhnt', qc, ns)
    h = num / jnp.maximum(jnp.abs(den), jnp.exp(-m_t))[..., None]
    return h.reshape(B, H, T, dv)


def _mixer(u, cos, sin, valid, w_in, q_norm, w_uq, kv_norm, w_ukv, b_igate, b_fgate, mlstm_norm,
           w_o_attn, w_o_mlstm, w_out):
    B, L, _ = u.shape
    z = u @ w_in
    points = np.cumsum(IN_SIZES)[:-1].tolist()
    z_q, z_kv, z_kr, m_q, m_k, m_v, m_o, z_i, z_f, z_ga, z_gb = jnp.split(z, points, axis=-1)

    q = (_rmsnorm(z_q, q_norm) @ w_uq).reshape(B, L, H_A, QK_DIM)
    q = jnp.concatenate([q[..., :NOPE], _rope(q[..., NOPE:], cos, sin)], axis=-1)
    kv = (_rmsnorm(z_kv, kv_norm) @ w_ukv).reshape(B, L, H_A, NOPE + V_DIM)
    k_r = _rope(z_kr[:, :, None, :], cos, sin)
    k = jnp.concatenate([kv[..., :NOPE], jnp.broadcast_to(k_r, (B, L, H_A, ROPE))], axis=-1)
    v = kv[..., NOPE:]
    attn = _block_attention(q.transpose(0, 2, 1, 3), k.transpose(0, 2, 1, 3), v.transpose(0, 2, 1, 3))
    attn = attn.transpose(0, 2, 1, 3).reshape(B, L, H_A * V_DIM)

    pad = CHUNK - N_META

    def heads(t):
        t = jnp.pad(t.reshape(B, L, H_B, DH_B), ((0, 0), (pad, 0), (0, 0), (0, 0)))
        return t.transpose(0, 2, 1, 3)

    mq, mk, mv = heads(m_q), heads(m_k), heads(m_v)
    gi = z_i.reshape(B, L, 2, H_B).astype(jnp.float32) + b_igate.astype(jnp.float32)
    gf = z_f.reshape(B, L, 2, H_B).astype(jnp.float32) + b_fgate.astype(jnp.float32)
    gi = jnp.pad(gi, ((0, 0), (pad, 0), (0, 0), (0, 0))).transpose(2, 0, 3, 1)
    gf = jnp.pad(gf, ((0, 0), (pad, 0), (0, 0), (0, 0))).transpose(2, 0, 3, 1)
    gi = jnp.where(valid, gi, NEG)
    log_f = jnp.where(valid, jax.nn.log_sigmoid(gf), 0.0)
    h_fwd = _mlstm_dir(mq, mk, mv, gi[0], log_f[0])
    h_bwd = jnp.flip(_mlstm_dir(jnp.flip(mq, 2), jnp.flip(mk, 2), jnp.flip(mv, 2),
                                jnp.flip(gi[1], -1), jnp.flip(log_f[1], -1)), 2)
    hm = (h_fwd + h_bwd)[:, :, pad:].transpose(0, 2, 1, 3).astype(u.dtype)
    hm = _rmsnorm(hm, mlstm_norm.reshape(H_B, DH_B)) * jax.nn.sigmoid(m_o).reshape(B, L, H_B, DH_B)
    hm = hm.reshape(B, L, H_B * DH_B)

    y = jax.nn.sigmoid(z_ga) * (attn @ w_o_attn) + jax.nn.sigmoid(z_gb) * (hm @ w_o_mlstm)
    return y @ w_out


def _conv_ffn(u, w_up, w_gate, conv_w, conv_b, w_down):
    L = u.shape[1]
    a = u @ w_up
    half = CONV_W // 2
    ap = jnp.pad(a, ((0, 0), (half, half), (0, 0)))
    a = sum(ap[:, j:j + L] * conv_w[j] for j in range(CONV_W)) + conv_b
    return (jax.nn.gelu(a, approximate=True) * (u @ w_gate)) @ w_down


def _encode(x, meta_tokens, norm_mix, w_in, q_norm, w_uq, kv_norm, w_ukv, b_igate, b_fgate, mlstm_norm,
            w_o_attn, w_o_mlstm, w_out, norm_ffn, w_up, w_gate, conv_w, conv_b, w_down, norm_final):
    B, S, _ = x.shape
    meta = jnp.broadcast_to(meta_tokens.astype(x.dtype)[None], (B, N_META, D_MODEL))
    h = jnp.concatenate([meta, x], axis=1)
    L = N_META + S
    cos, sin = _rope_tables(L)
    valid = jnp.arange(CHUNK + S) >= (CHUNK - N_META)
    for l in range(DEPTH):
        h = h + _mixer(_rmsnorm(h, norm_mix[l]), cos, sin, valid, w_in[l], q_norm[l], w_uq[l], kv_norm[l],
                       w_ukv[l], b_igate[l], b_fgate[l], mlstm_norm[l], w_o_attn[l], w_o_mlstm[l], w_out[l])
        h = h + _conv_ffn(_rmsnorm(h, norm_ffn[l]), w_up[l], w_gate[l], conv_w[l], conv_b[l], w_down[l])
    return _rmsnorm(h, norm_final)[:, N_META:]


def setup_inputs(seed: int = 0) -> dict:
    key = jax.random.key(seed)
    ks = jax.random.split(key, 24)
    f32 = jnp.float32

    def nrm(k, shape, fan):
        return jax.random.normal(k, shape, f32) * (fan ** -0.5)

    def gain(k, shape):
        return 1.0 + 0.01 * jax.random.normal(k, shape, f32)

    Ld = DEPTH
    return {
        "x_prompt": jax.random.normal(ks[0], (BATCH, SEQ, D_MODEL), f32),
        "x_sample": jax.random.normal(ks[1], (DEC_BATCH, DEC_SEQ, D_MODEL), f32),
        "meta_tokens": jax.random.normal(ks[2], (N_META, D_MODEL), f32),
        "norm_mix": gain(ks[3], (Ld, D_MODEL)),
        "w_in": nrm(ks[4], (Ld, D_MODEL, IN_COLS), D_MODEL),
        "q_norm": gain(ks[5], (Ld, Q_LORA)),
        "w_uq": nrm(ks[6], (Ld, Q_LORA, H_A * QK_DIM), Q_LORA),
        "kv_norm": gain(ks[7], (Ld, KV_LORA)),
        "w_ukv": nrm(ks[8], (Ld, KV_LORA, H_A * (NOPE + V_DIM)), KV_LORA),
        "b_igate": 0.1 * jax.random.normal(ks[9], (Ld, 2, H_B), f32),
        "b_fgate": jnp.linspace(3.0, 6.0, H_B, dtype=f32) + 0.1 * jax.random.normal(ks[10], (Ld, 2, H_B), f32),
        "mlstm_norm": gain(ks[11], (Ld, H_B * DH_B)),
        "w_o_attn": nrm(ks[12], (Ld, H_A * V_DIM, D_MODEL), H_A * V_DIM),
        "w_o_mlstm": nrm(ks[13], (Ld, H_B * DH_B, D_MODEL), H_B * DH_B),
        "w_out": nrm(ks[14], (Ld, D_MODEL, D_MODEL), D_MODEL),
        "norm_ffn": gain(ks[15], (Ld, D_MODEL)),
        "w_up": nrm(ks[16], (Ld, D_MODEL, D_FF), D_MODEL),
        "w_gate": nrm(ks[17], (Ld, D_MODEL, D_FF), D_MODEL),
        "conv_w": nrm(ks[18], (Ld, CONV_W, D_FF), CONV_W),
        "conv_b": 0.01 * jax.random.normal(ks[19], (Ld, D_FF), f32),
        "w_down": nrm(ks[20], (Ld, D_FF, D_MODEL), D_FF),
        "norm_final": gain(ks[21], (D_MODEL,)),
    }


def reference(x_prompt, x_sample, meta_tokens, norm_mix, w_in, q_norm, w_uq, kv_norm, w_ukv, b_igate, b_fgate,
              mlstm_norm, w_o_attn, w_o_mlstm, w_out, norm_ffn, w_up, w_gate, conv_w, conv_b, w_down, norm_final):
    y_prompt = _encode(x_prompt, meta_tokens, norm_mix, w_in, q_norm, w_uq, kv_norm, w_ukv, b_igate, b_fgate,
                       mlstm_norm, w_o_attn, w_o_mlstm, w_out, norm_ffn, w_up, w_gate, conv_w, conv_b, w_down,
                       norm_final)
    y_sample = _encode(x_sample, meta_tokens, norm_mix, w_in, q_norm, w_uq, kv_norm, w_ukv, b_igate, b_fgate,
                       mlstm_norm, w_o_attn, w_o_mlstm, w_out, norm_ffn, w_up, w_gate, conv_w, conv_b, w_down,
                       norm_final)
    return (y_prompt, y_sample)
```

```python
import numpy as np
import concourse.bass as bass
import concourse.mybir as mybir
from concourse.bass_utils import run_bass_kernel_spmd

F32 = mybir.dt.float32
BF16 = mybir.dt.bfloat16
AF = mybir.ActivationFunctionType
ALU = mybir.AluOpType

D = 1024
NMETA = 16
EPS = 1e-6
DFF = 2816
NFT = DFF // 128
NCORES = 8
NDS = 8
SCALE_A = 96 ** -0.5
SCALE_K = 128 ** -0.5
KBIAS_NEG = -30000.0
STRICT_SAME_ENGINE = True


class Buf:
    __slots__ = ("w", "r", "excl")

    def __init__(self):
        self.w = None
        self.r = {}
        self.excl = False


class T:
    def __init__(self, h, nb=1):
        self.h = h
        self.bs = [Buf() for _ in range(nb)]

    @property
    def b(self):
        return self.bs[0]

    def __getitem__(self, k):
        return self.h[k]


class Sched:
    ENG = ("pe", "act", "dve", "pool", "sp")

    def __init__(self, nc):
        self.nc = nc
        self.eng = {}
        for name in self.ENG:
            self.eng[name] = dict(sem=nc.alloc_semaphore("s_" + name), count=0, recs=[], seen={})
        self.dsem = {}
        for q in ("sp", "pool", "act"):
            self.dsem[q] = [[nc.alloc_semaphore("d_%s%d" % (q, i)), 0] for i in range(NDS)]
        self.drr = {"sp": 0, "pool": 0, "act": 0}
        self.snaps = {}
        self.nops = 0

    def semof(self, key):
        if isinstance(key, str):
            return self.eng[key]["sem"]
        return self.dsem[key[1]][key[2]][0]

    def _need(self, E, tok, waits):
        key, val = tok
        seen = self.eng[E]["seen"]
        if seen.get(key, 0) >= val:
            return
        if key == E and E == "pe":
            return
        waits.append(tok)
        seen[key] = val
        snap = self.snaps.get(tok)
        if snap:
            for k, v in snap.items():
                if seen.get(k, 0) < v:
                    seen[k] = v

    def _deps(self, E, r, w):
        waits = []
        for b in r:
            if b.w is not None:
                self._need(E, b.w, waits)
        for b in w:
            if b.w is not None and (STRICT_SAME_ENGINE or b.w[0] != E):
                self._need(E, b.w, waits)
            for k, v in b.r.items():
                if STRICT_SAME_ENGINE or k != E:
                    self._need(E, (k, v), waits)
        return waits

    def op(self, E, fn, r=(), w=()):
        e = self.eng[E]
        if any(b.excl for b in r):
            w = list(w) + [b for b in r if b.excl and b not in w]
            r = [b for b in r if not b.excl]
        waits = self._deps(E, r, w)
        e["count"] += 1
        tok = (E, e["count"])
        self.snaps[tok] = dict(e["seen"])
        for b in r:
            if b.r.get(E, 0) < tok[1]:
                b.r[E] = tok[1]
        for b in w:
            b.w = tok
            b.r = {}
        e["recs"].append((waits, fn, (e["sem"], 1)))
        self.nops += 1
        return tok

    def dma(self, q, out_ap, in_ap, r=(), w=()):
        e = self.eng[q]
        waits = self._deps(q, r, w)
        slot = self.drr[q]
        self.drr[q] = (slot + 1) % NDS
        rec = self.dsem[q][slot]
        key = ("d", q, slot)
        if rec[1] > 0:
            self._need(q, (key, rec[1]), waits)
        rec[1] += 16
        tok = (key, rec[1])
        self.snaps[tok] = dict(e["seen"])
        for b in r:
            if b.r.get(key, 0) < tok[1]:
                b.r[key] = tok[1]
        for b in w:
            b.w = tok
            b.r = {}
        e["recs"].append((waits, (lambda eng, o=out_ap, i=in_ap: eng.dma_start(out=o, in_=i)), (rec[0], 16)))
        return tok

    def all_tokens(self):
        toks = []
        for name in ("pe", "act", "dve", "pool"):
            c = self.eng[name]["count"]
            if c > 0:
                toks.append((name, c))
        for q in self.dsem:
            for i, rec in enumerate(self.dsem[q]):
                if rec[1] > 0:
                    toks.append((("d", q, i), rec[1]))
        return toks

    def barrier(self, engines=None):
        toks = self.all_tokens()
        for E in (engines or self.ENG):
            waits = []
            for tok in toks:
                if tok[0] == E:
                    continue
                self._need(E, tok, waits)
            if waits:
                self.eng[E]["recs"].append((waits, None, None))

    def emit(self):
        def run(name):
            def body(eng):
                for waits, fn, inc in self.eng[name]["recs"]:
                    for key, val in waits:
                        eng.wait_ge(self.semof(key), val)
                    if fn is not None:
                        ins = fn(eng)
                        ins.then_inc(inc[0], inc[1])
            return body

        with self.nc.Block() as block:
            block.sync(run("sp"))
            block.tensor(run("pe"))
            block.scalar(run("act"))
            block.vector(run("dve"))
            block.gpsimd(run("pool"))


class Arena:
    def __init__(self, nc):
        self.nc = nc
        self.base = (nc.sbuf_base + 63) // 64 * 64
        self.top = nc.sbuf_top
        self.cur = self.base
        self.n = 0
        self.limit = self.top
        self.relaxed = False
        self.peak = 0

    def alloc_top(self, shape, dtype, nb=1):
        esz = 2 if dtype == BF16 else 4
        n = esz
        for s_ in shape[1:]:
            n *= s_
        off = (self.limit - n) // 64 * 64
        self.limit = off
        self.n += 1
        h = self.nc.alloc_sbuf_tensor_at("t%d" % self.n, list(shape), dtype, offset=off)
        return T(h, nb)

    def mark(self):
        return self.cur

    def release(self, m):
        self.cur = m

    def alloc(self, shape, dtype, nb=1):
        esz = 2 if dtype == BF16 else 4
        n = esz
        for s in shape[1:]:
            n *= s
        off = self.cur
        self.cur = (off + n + 63) // 64 * 64
        lim = self.top if self.relaxed else self.limit
        assert self.cur <= lim, "SBUF overflow %d > %d" % (self.cur, lim)
        self.peak = max(self.peak, self.cur)
        self.n += 1
        h = self.nc.alloc_sbuf_tensor_at("t%d" % self.n, list(shape), dtype, offset=off)
        return T(h, nb)


def blocks_of(total, bs):
    out = []
    o = 0
    while o < total:
        out.append((o, min(bs, total - o)))
        o += bs
    return out


def build_program(NPS, NOWN, NKF):
    NSL = NPS + 1
    NCH = NOWN + 2
    L = 64 * NCH
    NT = L // 128
    NOWNT = 64 * NOWN
    nc = bass.Bass("TRN2", target_bir_lowering=False)

    def din(name, shape, dt=F32):
        return nc.dram_tensor(name, list(shape), dt, kind="ExternalInput").ap()

    xs = din("xs", [NSL, L, D])
    xfull = din("xfull", [NKF * 128, D])
    w_in = din("w_in", [D, 4528])
    w_uq = din("w_uq", [256, 768])
    w_ukv = din("w_ukv", [128, 1024])
    w_oa = din("w_oa", [512, D])
    w_om = din("w_om", [512, D])
    w_out = din("w_out", [D, D])
    w_up = din("w_up", [D, DFF])
    w_gate = din("w_gate", [D, DFF])
    w_down = din("w_down", [DFF, D])
    g_mix = din("g_mix", [128, 8])
    g_q = din("g_q", [128, 2])
    g_kv = din("g_kv", [128, 1])
    g_ml = din("g_ml", [128, 4])
    g_ffn = din("g_ffn", [128, 8])
    g_fin = din("g_fin", [128, D])
    bgate = din("bgate", [128, 16])
    convp = din("convp", [128, NFT, 4])
    valid128 = din("valid128", [NSL, 128, NT])
    valid64 = din("valid64", [NSL, 64, NCH])
    kbias_sl = din("kbias_sl", [NPS, 128, NT])
    kbias_full = din("kbias_full", [128, NKF])
    validfull = din("validfull", [128, NKF])
    act8 = din("act8", [128, NKF, 8])
    cs_tok = din("cs_tok", [NSL, 128, NT, 32])
    cs_full = din("cs_full", [128, NKF, 32])
    csT = din("csT", [NSL, 2, 32, L])
    cmat = din("cmat", [128, 5, 128])
    y = nc.dram_tensor("y", [NSL, NOWNT, D], F32, kind="ExternalOutput").ap()

    def dscr(name, shape, dt=BF16):
        return nc.dram_tensor(name, list(shape), dt).ap()

    s_win = dscr("s_win", [128, 8, 4528])
    s_wuq = dscr("s_wuq", [128, 2, 768])
    s_wuqr = dscr("s_wuqr", [128, 2, 768])
    s_wukv = dscr("s_wukv", [128, 1, 1024])
    s_woa = dscr("s_woa", [128, 4, D])
    s_wom = dscr("s_wom", [128, 4, D])
    s_wout = dscr("s_wout", [128, 8, D])
    s_wup = dscr("s_wup", [NFT, 128, 8, 128])
    s_wgate = dscr("s_wgate", [NFT, 128, 8, 128])
    s_wdown = dscr("s_wdown", [128, NFT, D])
    s_h2 = dscr("s_h2", [L, D], F32)
    bh2 = [Buf() for _ in range(NT)]
    bscr = Buf()

    import os
    KSUB = int(os.environ.get("KSUB", "0"))
    QB = int(os.environ.get("QB", "2"))
    S = Sched(nc)
    A = Arena(nc)
    P2 = [nc.alloc_psum_tensor("ps%d" % i, [128, 1024], F32) for i in range(3)]
    PS = []
    for k_ in range(3):
        PS.append(T(P2[k_][:, 0:512]))
        PS.append(T(P2[k_][:, 512:1024]))
    PB = [T(nc.alloc_psum_tensor("pb%d" % i, [128, 1024], BF16)) for i in range(2)]

    for t_ in PS + PB:
        t_.b.excl = True

    def psb(i, c0=0, n=1024):
        return PB[i].h[:, c0:c0 + n]

    def mm(out, lhsT, rhs, start, stop, r, w, skip=False):
        S.op("pe", lambda e, o=out, l=lhsT, rh=rhs, s=start, t=stop, k=skip:
             e.matmul(o, l, rh, start=s, stop=t, skip_group_check=k), r=r, w=w)

    def tr(out, in_, idn, r, w):
        S.op("pe", lambda e, o=out, i=in_, d=idn: e.transpose(o, i, d), r=r, w=w)

    def act(out, in_, func, r, w, bias=None, scale=None, accum=None, eng="act"):
        kw = {}
        if bias is not None:
            kw["bias"] = bias
        if scale is not None:
            kw["scale"] = scale
        if accum is not None:
            kw["accum_out"] = accum
        S.op("act", lambda e, o=out, i=in_, f=func, k=kw: e.activation(o, i, f, **k), r=r, w=w)

    def ts(out, in_, s1, s2, op0, op1, r, w, eng="dve"):
        if op1 is None:
            S.op(eng, lambda e, o=out, i=in_, a=s1, p=op0: e.tensor_scalar(o, i, a, None, p), r=r, w=w)
        else:
            S.op(eng, lambda e, o=out, i=in_, a=s1, b=s2, p=op0, q=op1: e.tensor_scalar(o, i, a, b, p, q), r=r, w=w)

    def tt(out, a, b, op, r, w, eng="dve"):
        S.op(eng, lambda e, o=out, x=a, z=b, p=op: e.tensor_tensor(o, x, z, p), r=r, w=w)

    def stt(out, in0, sc, in1, op0, op1, r, w):
        S.op("dve", lambda e, o=out, a=in0, s=sc, b=in1, p=op0, q=op1:
             e.scalar_tensor_tensor(o, a, s, b, p, q), r=r, w=w)

    def cp(out, in_, r, w, eng="dve"):
        if eng == "act":
            S.op("act", lambda e, o=out, i=in_: e.activation(o, i, AF.Copy), r=r, w=w)
        else:
            S.op(eng, lambda e, o=out, i=in_: e.tensor_copy(o, i), r=r, w=w)

    def memset(t_ap, val, w, eng="dve"):
        S.op(eng, lambda e, o=t_ap, v=val: e.memset(o, v), r=(), w=w)

    def recip(out, in_, r, w):
        S.op("dve", lambda e, o=out, i=in_: e.reciprocal(o, i), r=r, w=w)

    def rstd_from_ss(rs, ss, dim, r, w):
        act(rs, ss, AF.Ln, r=r, w=w, bias=EPS, scale=1.0 / dim)
        act(rs, rs, AF.Exp, r=w, w=w, scale=-0.5)

    ident = A.alloc([128, 128], BF16)
    cm = A.alloc([128, 5, 128], F32)
    ones_f = A.alloc([128, 128], F32)
    gfin = A.alloc([128, D], F32)
    bg = A.alloc([128, 16], F32)
    cvp = A.alloc([128, NFT, 4], F32)
    gm = A.alloc([128, 8], F32)
    gq = A.alloc([128, 2], F32)
    gkv = A.alloc([128, 1], F32)
    gml = A.alloc([128, 4], F32)
    gff = A.alloc([128, 8], F32)
    identf = A.alloc([128, 128], F32)
    Cf0 = A.alloc([128, 4, 129], F32)
    Cb0 = A.alloc([128, 4, 129], F32)
    S.dma("sp", cm[:], cmat, w=[cm.b])
    S.dma("sp", gfin[:], g_fin, w=[gfin.b])
    S.dma("sp", bg[:], bgate, w=[bg.b])
    S.dma("sp", cvp[:], convp, w=[cvp.b])
    S.dma("sp", gm[:], g_mix, w=[gm.b])
    S.dma("sp", gq[:], g_q, w=[gq.b])
    S.dma("sp", gkv[:], g_kv, w=[gkv.b])
    S.dma("sp", gml[:], g_ml, w=[gml.b])
    S.dma("sp", gff[:], g_ffn, w=[gff.b])
    memset(ones_f[:], 1.0, w=[ones_f.b])
    identsrc = din("identsrc", [128, 128])
    S.dma("sp", identf[:], identsrc, w=[identf.b])
    cp(ident[:], identf[:], r=[identf.b], w=[ident.b])
    cmb = A.alloc([128, 5, 128], BF16)
    ones_b = A.alloc([128, 128], BF16)
    cp(cmb[:], cm[:], r=[cm.b], w=[cmb.b])
    memset(ones_b[:], 1.0, w=[ones_b.b])
    UB_f64 = cmb.h[0:64, 0, 0:64]
    UB_b64 = cmb.h[0:64, 1, 0:64]
    UB128 = cmb.h[:, 4, :]
    U_f64 = cm.h[0:64, 0, 0:64]
    U_b64 = cm.h[0:64, 1, 0:64]
    MT_f = cm.h[0:64, 2, 0:64]
    MT_b = cm.h[0:64, 3, 0:64]
    U128 = cm.h[:, 4, :]

    m0 = A.mark()
    stg = [A.alloc([128, 1024], F32) for _ in range(3)]
    stb = [A.alloc([128, 1024], BF16) for _ in range(3)]
    cnt = [0]

    def prep(src, K, N, gain, dst_fn, post=None):
        for kt in range(K // 128):
            for c0, cb in blocks_of(N, 1024):
                i = cnt[0] % 3
                cnt[0] += 1
                a, bq = stg[i], stb[i]
                S.dma("sp", a.h[:, 0:cb], src[kt * 128:(kt + 1) * 128, c0:c0 + cb], w=[a.b])
                if gain is not None:
                    if cnt[0] % 2 == 0:
                        ts(bq.h[:, 0:cb], a.h[:, 0:cb], gain.h[:, kt:kt + 1], None, ALU.mult, None,
                           r=[a.b, gain.b], w=[bq.b])
                    else:
                        act(bq.h[:, 0:cb], a.h[:, 0:cb], AF.Copy, r=[a.b, gain.b], w=[bq.b], scale=gain.h[:, kt:kt + 1])
                else:
                    cp(bq.h[:, 0:cb], a.h[:, 0:cb], r=[a.b], w=[bq.b], eng=("dve" if cnt[0] % 2 == 0 else "act"))
                for (dst, lo, n) in dst_fn(kt, c0, cb):
                    S.dma("act", dst, bq.h[:, lo:lo + n], r=[bq.b], w=[bscr])

    prep(w_in, D, 4528, gm, lambda kt, c0, cb: [(s_win[:, kt, c0:c0 + cb], 0, cb)])
    prep(w_ukv, 128, 1024, gkv, lambda kt, c0, cb: [(s_wukv[:, kt, c0:c0 + cb], 0, cb)])
    prep(w_oa, 512, D, None, lambda kt, c0, cb: [(s_woa[:, kt, c0:c0 + cb], 0, cb)])
    prep(w_om, 512, D, gml, lambda kt, c0, cb: [(s_wom[:, kt, c0:c0 + cb], 0, cb)])
    prep(w_out, D, D, None, lambda kt, c0, cb: [(s_wout[:, kt, c0:c0 + cb], 0, cb)])

    def ffn_dst(sc):
        def f(kt, c0, cb):
            return [(sc[c0 // 128:(c0 + cb) // 128, :, kt, :].rearrange("f p c -> p f c"), 0, cb)]
        return f

    def prep_ffn(src, sc):
        for kt in range(8):
            for c0, cb in blocks_of(DFF, 1024):
                i = cnt[0] % 3
                cnt[0] += 1
                a, bq = stg[i], stb[i]
                S.dma("sp", a.h[:, 0:cb], src[kt * 128:(kt + 1) * 128, c0:c0 + cb], w=[a.b])
                if cnt[0] % 2 == 0:
                    ts(bq.h[:, 0:cb], a.h[:, 0:cb], gff.h[:, kt:kt + 1], None, ALU.mult, None,
                       r=[a.b, gff.b], w=[bq.b])
                else:
                    act(bq.h[:, 0:cb], a.h[:, 0:cb], AF.Copy, r=[a.b, gff.b], w=[bq.b], scale=gff.h[:, kt:kt + 1])
                S.dma("act", sc[c0 // 128:(c0 + cb) // 128, :, kt, :].rearrange("f p c -> p f c"),
                      bq.h[:, 0:cb].rearrange("p (f c) -> p f c", c=128), r=[bq.b], w=[bscr])

    prep_ffn(w_up, s_wup)
    prep_ffn(w_gate, s_wgate)
    prep(w_down, DFF, D, None, lambda kt, c0, cb: [(s_wdown[:, kt, c0:c0 + cb], 0, cb)])
    for kt in range(2):
        i = cnt[0] % 3
        cnt[0] += 1
        a, bq = stg[i], stb[i]
        S.dma("sp", a.h[:, 0:768], w_uq[kt * 128:(kt + 1) * 128, :], w=[a.b])
        ts(bq.h[:, 0:768], a.h[:, 0:768], gq.h[:, kt:kt + 1], None, ALU.mult, None, r=[a.b, gq.b], w=[bq.b])
        S.dma("act", s_wuq[:, kt, :], bq.h[:, 0:768], r=[bq.b], w=[bscr])
        i2 = cnt[0] % 3
        cnt[0] += 1
        b2 = stb[i2]
        a3 = a.h[:, 0:768].rearrange("p (h c) -> p h c", c=96)
        o3 = b2.h[:, 0:768].rearrange("p (h c) -> p h c", c=96)
        ts(b2.h[:, 0:768], a.h[:, 0:768], gq.h[:, kt:kt + 1], None, ALU.mult, None, r=[a.b, gq.b], w=[b2.b])
        ts(o3[:, :, 64:80], a3[:, :, 80:96], gq.h[:, kt:kt + 1], -1.0, ALU.mult, ALU.mult, r=[a.b, gq.b], w=[b2.b])
        ts(o3[:, :, 80:96], a3[:, :, 64:80], gq.h[:, kt:kt + 1], None, ALU.mult, None, r=[a.b, gq.b], w=[b2.b])
        S.dma("act", s_wuqr[:, kt, :], b2.h[:, 0:768], r=[b2.b], w=[bscr])
    S.barrier()
    A.release(m0)

    def front_tile(src_rows, xt, ubf, ssb, rsb, junk, psbank, ut_out, ut_bufs, validcol=None):
        S.dma("sp", xt[:], src_rows, w=[xt.b])
        act(junk[:], xt[:], AF.Square, r=[xt.b], w=[junk.b, ssb.b], accum=ssb.h[:, 0:1])
        act(rsb.h[:, 0:1], ssb.h[:, 0:1], AF.Ln, r=[ssb.b], w=[rsb.b], bias=EPS, scale=1.0 / D)
        act(rsb.h[:, 0:1], rsb.h[:, 0:1], AF.Exp, r=[rsb.b], w=[rsb.b], scale=-0.5)
        ts(ubf[:], xt[:], rsb.h[:, 0:1], None, ALU.mult, None, r=[xt.b, rsb.b], w=[ubf.b])
        if KSUB == 1:
            return
        for kt in range(8):
            tr(psb(0, kt * 128, 128), ubf.h[:, kt * 128:(kt + 1) * 128], ident[:],
               r=[ubf.b, ident.b], w=[PB[0].b])
        cp(ut_out, psb(0, 0, 1024).rearrange("p (k t) -> p k t", k=8), r=[PB[0].b], w=ut_bufs)

    uT = A.alloc_top([128, 8, L], BF16, nb=NT)
    m_slice = A.mark()
    qnT = hmT = attnT = None

    def scan_pass():
        m = A.mark()
        xb = [A.alloc([128, D], F32) for _ in range(2)]
        ubf = [A.alloc([128, D], BF16) for _ in range(2)]
        junk = A.alloc([128, D], BF16)
        ssb = [A.alloc([128, 2], F32) for _ in range(2)]
        rsb = [A.alloc([128, 2], F32) for _ in range(2)]
        ut = [A.alloc([128, 8, 128], BF16) for _ in range(2)]
        wpre = A.alloc([128, 8, 1040], BF16)
        a8 = A.alloc([128, NKF, 8], F32)
        vfull = A.alloc([128, NKF], F32)
        vaug = [A.alloc([128, 4, 130], BF16) for _ in range(2)]
        kw = [A.alloc([128, 128], BF16) for _ in range(4)]
        zg = A.alloc([128, 16], F32)
        lf = A.alloc([128, 8], F32)
        e8 = A.alloc([128, 8], F32)
        al8 = A.alloc([128, 8], F32)
        arg = A.alloc([128, 8], F32)
        gam = A.alloc([128, 4], F32)
        cs16 = A.alloc([128, 16], F32)
        lfh = A.alloc([128, 16], BF16)
        lfr = A.alloc([128, 8], F32)
        arg2 = A.alloc([128, 8], F32)
        S.dma("sp", wpre.h[:, :, 0:16], s_win[:, :, 2464:2480], r=[bscr], w=[wpre.b])
        S.dma("sp", wpre.h[:, :, 16:528], s_win[:, :, 928:1440], r=[bscr], w=[wpre.b])
        S.dma("sp", wpre.h[:, :, 528:1040], s_win[:, :, 1440:1952], r=[bscr], w=[wpre.b])
        S.dma("sp", a8[:], act8, w=[a8.b])
        S.dma("sp", vfull[:], validfull, w=[vfull.b])
        memset(Cf0[:], 0.0, w=[Cf0.b])
        memset(Cb0[:], 0.0, w=[Cb0.b])
        memset(gam[:], 1.0, w=[gam.b])
        for v in vaug:
            memset(v[:], 1.0, w=[v.b])
        ksb = [A.alloc([128, 512], BF16) for _ in range(3)]
        va3 = [A.alloc([128, 4, 130], BF16) for _ in range(3)]
        for v in va3:
            memset(v[:], 1.0, w=[v.b])
        e8s = [A.alloc([128, 8], F32) for _ in range(2)]
        al8s = [A.alloc([128, 8], F32) for _ in range(2)]
        zgs = [A.alloc([128, 16], F32) for _ in range(2)]
        lfs = [A.alloc([128, 8], F32) for _ in range(2)]
        lfhs = [A.alloc([128, 16], BF16) for _ in range(2)]

        def stageA(j):
            p = j % 2
            front_tile(xfull[j * 128:(j + 1) * 128, :], xb[p], ubf[p], ssb[p], rsb[p], junk, 0,
                       ut[p][:], [ut[p].b])

        def stageBmm(j):
            p = j % 2
            u = ut[p]
            pg_ = PS[p]
            for kt in range(8):
                mm(pg_.h[:, 0:16], u.h[:, kt, :], wpre.h[:, kt, 0:16], kt == 0, kt == 7, r=[u.b, wpre.b], w=[pg_.b])
            for kt in range(8):
                mm(PS[2].h[:, :], u.h[:, kt, :], wpre.h[:, kt, 16:528], kt == 0, kt == 7, r=[u.b, wpre.b], w=[PS[2].b])
            for kt in range(8):
                mm(PS[3].h[:, :], u.h[:, kt, :], wpre.h[:, kt, 528:1040], kt == 0, kt == 7, r=[u.b, wpre.b], w=[PS[3].b])

        def stageBrest(j):
            p = j % 2
            q3 = j % 3
            pg_ = PS[p]
            zg_, lf_, lfh_ = zgs[p], lfs[p], lfhs[p]
            act(ksb[q3][:], PS[2].h[:, :], AF.Copy, r=[PS[2].b], w=[ksb[q3].b], scale=SCALE_K)
            cp(va3[q3].h[:, :, 0:128], PS[3].h[:, :].rearrange("p (h c) -> p h c", h=4), r=[PS[3].b], w=[va3[q3].b])
            tt(zg_[:], pg_.h[:, 0:16], bg[:], ALU.add, r=[pg_.b, bg.b], w=[zg_.b])
            act(lf_[:], zg_.h[:, 8:16], AF.Exp, r=[zg_.b], w=[lf_.b], scale=-1.0)
            act(lf_[:], lf_[:], AF.Ln, r=[lf_.b], w=[lf_.b], bias=1.0)
            ts(lf_[:], lf_[:], vfull.h[:, j:j + 1], -1.0, ALU.mult, ALU.mult, r=[lf_.b, vfull.b], w=[lf_.b])
            cp(lfh_.h[:, 0:8], lf_[:], r=[lf_.b], w=[lfh_.b])
            tt(lfr[:], lf_[:], lfh_.h[:, 0:8], ALU.subtract, r=[lf_.b, lfh_.b], w=[lfr.b])
            cp(lfh_.h[:, 8:16], lfr[:], r=[lfr.b], w=[lfh_.b])

        def stage2(j):
            p = j % 2
            pg_ = PS[p]
            zg_, lf_, lfh_ = zgs[p], lfs[p], lfhs[p]
            mm(pg_.h[:, 16:24], UB128, lfh_.h[:, 0:8], True, False, r=[cmb.b, lfh_.b], w=[pg_.b])
            mm(pg_.h[:, 16:24], UB128, lfh_.h[:, 8:16], False, True, r=[cmb.b, lfh_.b], w=[pg_.b])
            mm(pg_.h[:, 24:32], ones_b[:], lfh_.h[:, 0:8], True, False, r=[ones_b.b, lfh_.b], w=[pg_.b])
            mm(pg_.h[:, 24:32], ones_b[:], lfh_.h[:, 8:16], False, True, r=[ones_b.b, lfh_.b], w=[pg_.b])
            cp(cs16[:], pg_.h[:, 16:32], r=[pg_.b], w=[cs16.b])
            tt(arg.h[:, 0:4], cs16.h[:, 8:12], cs16.h[:, 0:4], ALU.subtract, r=[cs16.b], w=[arg.b])
            tt(arg.h[:, 4:8], cs16.h[:, 4:8], lf_.h[:, 4:8], ALU.subtract, r=[cs16.b, lf_.b], w=[arg.b])
            tt(arg[:], arg[:], zg_.h[:, 0:8], ALU.add, r=[arg.b, zg_.b], w=[arg.b])
            act(e8s[p][:], arg[:], AF.Exp, r=[arg.b], w=[e8s[p].b])
            tt(e8s[p][:], e8s[p][:], a8.h[:, j, :], ALU.mult, r=[e8s[p].b, a8.b], w=[e8s[p].b])
            tt(arg2[:], cs16.h[:, 8:16], a8.h[:, j, :], ALU.mult, r=[cs16.b, a8.b], w=[arg2.b])
            act(al8s[p][:], arg2[:], AF.Exp, r=[arg2.b], w=[al8s[p].b])

        kw8 = [A.alloc([128, 128], BF16) for _ in range(8)]

        def stage3(j):
            p = j % 2
            q3 = j % 3
            va = va3[q3]
            e8_, al8_ = e8s[p], al8s[p]
            for hd in range(8):
                h = hd % 4
                ts(kw8[hd][:], ksb[q3].h[:, h * 128:(h + 1) * 128], e8_.h[:, hd:hd + 1], None, ALU.mult, None,
                   r=[ksb[q3].b, e8_.b], w=[kw8[hd].b])

        def stage3rest(j):
            p = j % 2
            q3 = j % 3
            va = va3[q3]
            e8_, al8_ = e8s[p], al8s[p]
            for hd in range(8):
                h = hd % 4
                pd_ = PS[2 + (hd % 4)]
                mm(pd_.h[:, 0:129], kw8[hd][:], va.h[:, h, 0:129], True, True, r=[kw8[hd].b, va.b], w=[pd_.b])
                if hd < 4:
                    stt(Cf0.h[:, h, :], Cf0.h[:, h, :], al8_.h[:, h:h + 1], pd_.h[:, 0:129], ALU.mult, ALU.add,
                        r=[Cf0.b, al8_.b, pd_.b], w=[Cf0.b])
                else:
                    stt(Cb0.h[:, h, :], pd_.h[:, 0:129], gam.h[:, h:h + 1], Cb0.h[:, h, :], ALU.mult, ALU.add,
                        r=[Cb0.b, gam.b, pd_.b], w=[Cb0.b])
            tt(gam[:], gam[:], al8_.h[:, 4:8], ALU.mult, r=[gam.b, al8_.b], w=[gam.b])

        for i in range(NKF + 3):
            if 0 <= i - 3 < NKF:
                stage3(i - 3)
            if 0 <= i - 2 < NKF:
                stage2(i - 2)
            if 0 <= i - 1 < NKF:
                stageBmm(i - 1)
            if i < NKF:
                stageA(i)
            if 0 <= i - 1 < NKF:
                stageBrest(i - 1)
            if 0 <= i - 3 < NKF:
                stage3rest(i - 3)
        S.barrier()
        A.release(m)

    def phase0(sl, kvdst=None):
        m = A.mark()
        xb = [A.alloc([128, D], F32) for _ in range(2)]
        ubf = [A.alloc([128, D], BF16) for _ in range(2)]
        junk = A.alloc([128, D], BF16)
        ssb = [A.alloc([128, 2], F32) for _ in range(2)]
        rsb = [A.alloc([128, 2], F32) for _ in range(2)]
        NW = 416 if kvdst is not None else 256
        wl = A.alloc([128, 8, NW], BF16)
        qn = [A.alloc([128, 256], BF16) for _ in range(2)]
        S.dma("sp", wl[:], s_win[:, :, 0:NW], r=[bscr], w=[wl.b])
        if kvdst is not None:
            KT_, cnT_ = kvdst
            ssk = [A.alloc([128, 2], F32) for _ in range(2)]
            cst = A.alloc([128, NT, 32], F32)
            cn = [A.alloc([128, 128], BF16) for _ in range(2)]
            kst = [A.alloc([128, 96], BF16) for _ in range(2)]
            tmp = A.alloc([128, 4, 16], F32)
            S.dma("sp", cst[:], cs_tok[sl], w=[cst.b])
            for k in kst:
                memset(k[:], 0.0, w=[k.b])
        def p0A(i):
            p = i % 2
            tsl = slice(i * 128, (i + 1) * 128)
            front_tile(xs[sl, tsl, :], xb[p], ubf[p], ssb[p], rsb[p], junk, 0, uT.h[:, :, tsl], [uT.bs[i]])

        def p0mm(i):
            p = i % 2
            tsl = slice(i * 128, (i + 1) * 128)
            pl = PS[1 + p]
            for kt in range(8):
                mm(pl.h[:, 0:NW], uT.h[:, kt, tsl], wl.h[:, kt, :], kt == 0, kt == 7,
                   r=[uT.bs[i], wl.b], w=[pl.b])

        def p0B(i):
            p = i % 2
            pl = PS[1 + p]
            act(junk.h[:, 0:256], pl.h[:, 0:256], AF.Square, r=[pl.b], w=[junk.b, ssq[p].b],
                accum=ssq[p].h[:, 0:1])
            rstd_from_ss(ssq[p].h[:, 1:2], ssq[p].h[:, 0:1], 256, r=[ssq[p].b], w=[ssq[p].b])
            act(qn[p][:], pl.h[:, 0:256], AF.Copy, r=[pl.b, ssq[p].b], w=[qn[p].b], scale=ssq[p].h[:, 1:2])
            if kvdst is not None:
                act(junk.h[:, 256:384], pl.h[:, 256:384], AF.Square, r=[pl.b], w=[junk.b, ssk[p].b],
                    accum=ssk[p].h[:, 0:1])
                rstd_from_ss(ssk[p].h[:, 1:2], ssk[p].h[:, 0:1], 128, r=[ssk[p].b], w=[ssk[p].b])
                act(cn[p][:], pl.h[:, 256:384], AF.Copy, r=[pl.b, ssk[p].b], w=[cn[p].b], scale=ssk[p].h[:, 1:2])
                x1 = pl.h[:, 384:400]
                x2 = pl.h[:, 400:416]
                co = cst.h[:, i, 0:16]
                si = cst.h[:, i, 16:32]
                tt(tmp.h[:, 0, :], x1, co, ALU.mult, r=[pl.b, cst.b], w=[tmp.b])
                tt(tmp.h[:, 1, :], x2, si, ALU.mult, r=[pl.b, cst.b], w=[tmp.b])
                tt(tmp.h[:, 2, :], x1, si, ALU.mult, r=[pl.b, cst.b], w=[tmp.b])
                tt(tmp.h[:, 3, :], x2, co, ALU.mult, r=[pl.b, cst.b], w=[tmp.b])
                tt(kst[p].h[:, 64:80], tmp.h[:, 0, :], tmp.h[:, 1, :], ALU.subtract, r=[tmp.b], w=[kst[p].b])
                tt(kst[p].h[:, 80:96], tmp.h[:, 2, :], tmp.h[:, 3, :], ALU.add, r=[tmp.b], w=[kst[p].b])

        def p0C(i):
            p = i % 2
            tsl = slice(i * 128, (i + 1) * 128)
            for kt in range(2):
                tr(psb(1, kt * 128, 128), qn[p].h[:, kt * 128:(kt + 1) * 128], ident[:],
                   r=[qn[p].b, ident.b], w=[PB[1].b])
            if kvdst is not None:
                tr(psb(1, 256, 128), cn[p][:], ident[:], r=[cn[p].b, ident.b], w=[PB[1].b])
                tr(PB[1].h[0:96, 384:512], kst[p][:], ident[:], r=[kst[p].b, ident.b], w=[PB[1].b])
            cp(qnT.h[:, :, tsl], psb(1, 0, 256).rearrange("p (k t) -> p k t", k=2), r=[PB[1].b], w=[qnT.bs[i]])
            if kvdst is not None:
                cp(cnT_.h[:, tsl], psb(1, 256, 128), r=[PB[1].b], w=[cnT_.bs[i]])
                cp(KT_.h[64:96, tsl], PB[1].h[64:96, 384:512], r=[PB[1].b], w=[KT_.bs[i]], eng="act")

        ssq = [A.alloc([128, 2], F32) for _ in range(2)]
        for it in range(NT + 2):
            if 0 <= it - 2 < NT:
                p0C(it - 2)
            if 0 <= it - 1 < NT:
                p0mm(it - 1)
            if it < NT:
                p0A(it)
            if 0 <= it - 1 < NT:
                p0B(it - 1)
        S.barrier()
        A.release(m)

    def kv_pass(src_fn, ntiles, cs_src, KT, cnT):
        m = A.mark()
        xb = [A.alloc([128, D], F32) for _ in range(2)]
        ubf = [A.alloc([128, D], BF16) for _ in range(2)]
        junk = A.alloc([128, D], BF16)
        ssb = [A.alloc([128, 2], F32) for _ in range(2)]
        rsb = [A.alloc([128, 2], F32) for _ in range(2)]
        ut = [A.alloc([128, 8, 128], BF16) for _ in range(2)]
        wkv = A.alloc([128, 8, 160], BF16)
        cst = A.alloc([128, ntiles, 32], F32)
        cn = [A.alloc([128, 128], BF16) for _ in range(2)]
        kst = [A.alloc([128, 96], BF16) for _ in range(2)]
        tmp = A.alloc([128, 4, 16], F32)
        S.dma("sp", wkv[:], s_win[:, :, 256:416], r=[bscr], w=[wkv.b])
        S.dma("sp", cst[:], cs_src, w=[cst.b])
        for k in kst:
            memset(k[:], 0.0, w=[k.b])
        def kstage1(j):
            p = j % 2
            front_tile(src_fn(j), xb[p], ubf[p], ssb[p], rsb[p], junk, 0, ut[p][:], [ut[p].b])

        def kstage2mm(j):
            p = j % 2
            u = ut[p]
            pk_ = PS[1 + p]
            for kt in range(8):
                mm(pk_.h[:, 0:160], u.h[:, kt, :], wkv.h[:, kt, :], kt == 0, kt == 7, r=[u.b, wkv.b], w=[pk_.b])

        def kstage2(j):
            p = j % 2
            act(junk.h[:, 0:128], PS[1 + p].h[:, 0:128], AF.Square, r=[PS[1 + p].b], w=[junk.b, ssb[p].b],
                accum=ssb[p].h[:, 1:2])
            rstd_from_ss(rsb[p].h[:, 1:2], ssb[p].h[:, 1:2], 128, r=[ssb[p].b], w=[rsb[p].b])
            act(cn[p][:], PS[1 + p].h[:, 0:128], AF.Copy, r=[PS[1 + p].b, rsb[p].b], w=[cn[p].b], scale=rsb[p].h[:, 1:2])
            x1 = PS[1 + p].h[:, 128:144]
            x2 = PS[1 + p].h[:, 144:160]
            co = cst.h[:, j, 0:16]
            si = cst.h[:, j, 16:32]
            tt(tmp.h[:, 0, :], x1, co, ALU.mult, r=[PS[1 + p].b, cst.b], w=[tmp.b])
            tt(tmp.h[:, 1, :], x2, si, ALU.mult, r=[PS[1 + p].b, cst.b], w=[tmp.b])
            tt(tmp.h[:, 2, :], x1, si, ALU.mult, r=[PS[1 + p].b, cst.b], w=[tmp.b])
            tt(tmp.h[:, 3, :], x2, co, ALU.mult, r=[PS[1 + p].b, cst.b], w=[tmp.b])
            tt(kst[p].h[:, 64:80], tmp.h[:, 0, :], tmp.h[:, 1, :], ALU.subtract, r=[tmp.b], w=[kst[p].b])
            tt(kst[p].h[:, 80:96], tmp.h[:, 2, :], tmp.h[:, 3, :], ALU.add, r=[tmp.b], w=[kst[p].b])

        def kstage3(j):
            p = j % 2
            tr(psb(1, 0, 128), cn[p][:], ident[:], r=[cn[p].b, ident.b], w=[PB[1].b])
            tr(PB[1].h[0:96, 128:256], kst[p][:], ident[:], r=[kst[p].b, ident.b], w=[PB[1].b])
            cp(cnT.h[:, j * 128:(j + 1) * 128], psb(1, 0, 128), r=[PB[1].b], w=[cnT.bs[j]])
            cp(KT.h[64:96, j * 128:(j + 1) * 128], PB[1].h[64:96, 128:256], r=[PB[1].b], w=[KT.bs[j]], eng="act")

        for i in range(ntiles + 2):
            if 0 <= i - 2 < ntiles:
                kstage3(i - 2)
            if 0 <= i - 1 < ntiles:
                kstage2mm(i - 1)
            if i < ntiles:
                kstage1(i)
            if 0 <= i - 1 < ntiles:
                kstage2(i - 1)
        S.barrier()
        A.release(m)

    AXX = mybir.AxisListType.X

    def phase1(sl, sample):
        m = A.mark()
        wq2 = [A.alloc([128, 8, 128], BF16) for _ in range(2)]
        wk2 = [A.alloc([128, 8, 128], BF16) for _ in range(2)]
        wx2 = [A.alloc([128, 8, 388], BF16) for _ in range(2)]

        def load_head_w(hh):
            wq_, wk_, wx_ = wq2[hh % 2], wk2[hh % 2], wx2[hh % 2]
            S.dma("sp", wq_[:], s_win[:, :, 416 + 128 * hh:544 + 128 * hh], r=[bscr], w=[wq_.b])
            S.dma("sp", wk_[:], s_win[:, :, 928 + 128 * hh:1056 + 128 * hh], r=[bscr], w=[wk_.b])
            S.dma("sp", wx_.h[:, :, 0:128], s_win[:, :, 928 + 128 * hh:1056 + 128 * hh], r=[bscr], w=[wx_.b])
            S.dma("sp", wx_.h[:, :, 128:256], s_win[:, :, 1440 + 128 * hh:1568 + 128 * hh], r=[bscr], w=[wx_.b])
            S.dma("sp", wx_.h[:, :, 256:384], s_win[:, :, 1952 + 128 * hh:2080 + 128 * hh], r=[bscr], w=[wx_.b])
        wg16 = A.alloc([128, 8, 16], BF16)
        qT = A.alloc([128, L], BF16)
        kT = A.alloc([128, L], BF16)
        kh = A.alloc([64, NCH, 128], BF16, nb=NCH)
        va = A.alloc([64, NCH, 130], BF16, nb=NCH)
        so = A.alloc([64, NCH, 128], BF16, nb=NCH)
        Cbf = A.alloc([128, NCH + 1, 130], BF16, nb=NCH + 1)
        Cbb = A.alloc([128, NCH + 1, 130], BF16, nb=NCH + 1)
        Cf = A.alloc([128, 129], F32)
        Cb = A.alloc([128, 129], F32)
        zall = A.alloc([64, NCH, 4], F32, nb=NCH)
        lf3 = A.alloc([64, NCH, 2], F32)
        lh3 = A.alloc([64, 2, NCH, 2], BF16)
        res3 = A.alloc([64, NCH, 2], F32)
        bb3 = A.alloc([64, NCH, 2], F32)
        d3 = A.alloc([64, NCH, 2], F32)
        ew3 = A.alloc([64, NCH, 2], F32)
        eb3 = A.alloc([64, NCH, 2], F32)
        ew23 = A.alloc([64, NCH, 2], F32)
        eg3 = A.alloc([128, NCH, 2], F32)
        v64 = A.alloc([64, NCH], F32)
        kw = [A.alloc([64, 128], BF16) for _ in range(6)]
        sp_ = [A.alloc([64, 2, 64], BF16) for _ in range(3)]
        stage = A.alloc([64, NCH, 2, 130], F32, nb=NCH)
        den3 = A.alloc([64, NCH, 2], F32)
        neg3 = A.alloc([64, NCH, 2], F32)
        ss3 = A.alloc([64, NCH], F32)
        bgh = A.alloc([64, 4], F32)
        S.dma("sp", v64[:], valid64[sl], w=[v64.b])
        S.dma("sp", wg16[:], s_win[:, :, 2464:2480], r=[bscr], w=[wg16.b])
        NC2 = 2 * NCH
        for h in range(4):
            wq, wk, wx = wq2[h % 2], wk2[h % 2], wx2[h % 2]
            if h == 0:
                load_head_w(0)
            cp(wx.h[:, :, 384:388], wg16.h[:, :, h:16:4], r=[wg16.b], w=[wx.b], eng="pool")
            cp(bgh[:], bg.h[0:64, h:16:4], r=[bg.b], w=[bgh.b], eng="pool")
            memset(va.h[:, :, 128:130], 1.0, w=va.bs)
            for (t0, tn) in blocks_of(L, 512):
                tl = list(range(t0 // 128, (t0 + tn) // 128))
                for kt in range(8):
                    mm(PS[0].h[:, 0:tn], wq.h[:, kt, :], uT.h[:, kt, t0:t0 + tn], kt == 0, kt == 7,
                       r=[wq.b] + [uT.bs[i] for i in tl], w=[PS[0].b])
                cp(qT.h[:, t0:t0 + tn], PS[0].h[:, 0:tn], r=[PS[0].b], w=[qT.b], eng="act")
                for kt in range(8):
                    mm(PS[1].h[:, 0:tn], wk.h[:, kt, :], uT.h[:, kt, t0:t0 + tn], kt == 0, kt == 7,
                       r=[wk.b] + [uT.bs[i] for i in tl], w=[PS[1].b])
                ts(kT.h[:, t0:t0 + tn], PS[1].h[:, 0:tn], SCALE_K, None, ALU.mult, None, r=[PS[1].b], w=[kT.b])
            if sample:
                cp(Cf[:], Cf0.h[:, h, :], r=[Cf0.b], w=[Cf.b])
                cp(Cb[:], Cb0.h[:, h, :], r=[Cb0.b], w=[Cb.b])
            else:
                memset(Cf[:], 0.0, w=[Cf.b])
                memset(Cb[:], 0.0, w=[Cb.b])
            cp(Cbf.h[:, 0, 0:129], Cf[:], r=[Cf.b], w=[Cbf.bs[0]], eng="act")
            cp(Cbb.h[:, NCH, 0:129], Cb[:], r=[Cb.b], w=[Cbb.bs[NCH]], eng="act")
            for n in range(NCH):
                cs_ = slice(n * 64, (n + 1) * 64)
                pa = PS[2 + (n % 4)]
                for kt in range(8):
                    mm(pa.h[0:64, 0:388], uT.h[:, kt, cs_], wx.h[:, kt, :], kt == 0, kt == 7,
                       r=[uT.bs[n // 2], wx.b], w=[pa.b])
                act(so.h[:, n, :], pa.h[0:64, 256:384], AF.Sigmoid, r=[pa.b], w=[so.bs[n]])
                ts(kh.h[:, n, :], pa.h[0:64, 0:128], SCALE_K, None, ALU.mult, None, r=[pa.b], w=[kh.bs[n]])
                cp(va.h[:, n, 0:128], pa.h[0:64, 128:256], r=[pa.b], w=[va.bs[n]])
                tt(zall.h[:, n, :], pa.h[0:64, 384:388], bgh[:], ALU.add, r=[pa.b, bgh.b], w=[zall.bs[n]])
            if h + 1 < 4:
                load_head_w(h + 1)
            zb = zall.bs
            act(lf3[:], zall.h[:, :, 2:4], AF.Exp, r=zb, w=[lf3.b], scale=-1.0)
            act(lf3[:], lf3[:], AF.Ln, r=[lf3.b], w=[lf3.b], bias=1.0)
            ts(lf3[:], lf3[:], -1.0, None, ALU.mult, None, r=[lf3.b], w=[lf3.b])
            tt(lf3[:], lf3[:], v64.h[:, :].unsqueeze(2).broadcast_to([64, NCH, 2]), ALU.mult, r=[lf3.b, v64.b], w=[lf3.b])
            cp(lh3.h[:, 0, :, :], lf3[:], r=[lf3.b], w=[lh3.b])
            tt(res3[:], lf3[:], lh3.h[:, 0, :, :], ALU.subtract, r=[lf3.b, lh3.b], w=[res3.b])
            cp(lh3.h[:, 1, :, :], res3[:], r=[res3.b], w=[lh3.b])
            hi2 = lh3.h[:, 0, :, :].rearrange("p n c -> p (n c)")
            lo2 = lh3.h[:, 1, :, :].rearrange("p n c -> p (n c)")
            pc = PS[0]
            mm(pc.h[0:64, 0:NC2], UB_f64, hi2, True, False, r=[cmb.b, lh3.b], w=[pc.b])
            mm(pc.h[0:64, 0:NC2], UB_f64, lo2, False, True, r=[cmb.b, lh3.b], w=[pc.b])
            mm(pc.h[0:64, NC2:2 * NC2], UB_b64, hi2, True, False, r=[cmb.b, lh3.b], w=[pc.b])
            mm(pc.h[0:64, NC2:2 * NC2], UB_b64, lo2, False, True, r=[cmb.b, lh3.b], w=[pc.b])
            mm(pc.h[:, 2 * NC2:3 * NC2], ones_b.h[0:64, :], hi2, True, False, r=[ones_b.b, lh3.b], w=[pc.b])
            mm(pc.h[:, 2 * NC2:3 * NC2], ones_b.h[0:64, :], lo2, False, True, r=[ones_b.b, lh3.b], w=[pc.b])
            pcF = pc.h[0:64, 0:NC2].rearrange("p (n c) -> p n c", c=2)
            pcB = pc.h[0:64, NC2:2 * NC2].rearrange("p (n c) -> p n c", c=2)
            pcT = pc.h[:, 2 * NC2:3 * NC2].rearrange("p (n c) -> p n c", c=2)
            cp(bb3.h[:, :, 0:1], pcF[:, :, 0:1], r=[pc.b], w=[bb3.b])
            cp(bb3.h[:, :, 1:2], pcB[:, :, 1:2], r=[pc.b], w=[bb3.b])
            act(eg3[:], pcT, AF.Exp, r=[pc.b], w=[eg3.b])
            tt(d3[:], zall.h[:, :, 0:2], bb3[:], ALU.subtract, r=zb + [bb3.b], w=[d3.b])
            act(ew3[:], d3[:], AF.Exp, r=[d3.b], w=[ew3.b])
            act(eb3[:], bb3[:], AF.Exp, r=[bb3.b], w=[eb3.b])
            tt(ew23[:], ew3[:], eg3.h[0:64, :, :], ALU.mult, r=[ew3.b, eg3.b], w=[ew23.b])
            for i in range(NCH + 1):
                if i < NCH:
                    nf, nb_ = i, NCH - 1 - i
                    kf = kw[(2 * i) % 6]
                    kb = kw[(2 * i + 1) % 6]
                    ts(kf[:], kh.h[:, nf, :], ew23.h[:, nf, 0:1], None, ALU.mult, None, r=[kh.bs[nf], ew23.b], w=[kf.b])
                    ts(kb[:], kh.h[:, nb_, :], ew23.h[:, nb_, 1:2], None, ALU.mult, None, r=[kh.bs[nb_], ew23.b], w=[kb.b])
                    pdf = PS[1 + (i % 2)]
                    pdb = PS[3 + (i % 2)]
                    mm(pdf.h[:, 0:129], kf[:], va.h[:, nf, 0:129], True, True, r=[kf.b, va.bs[nf]], w=[pdf.b])
                    mm(pdb.h[:, 0:129], kb[:], va.h[:, nb_, 0:129], True, True, r=[kb.b, va.bs[nb_]], w=[pdb.b])
                j = i - 1
                if j >= 0:
                    nf, nb_ = j, NCH - 1 - j
                    pdf = PS[1 + (j % 2)]
                    pdb = PS[3 + (j % 2)]
                    stt(Cf[:], Cf[:], eg3.h[:, nf, 0:1], pdf.h[:, 0:129], ALU.mult, ALU.add, r=[Cf.b, eg3.b, pdf.b], w=[Cf.b])
                    stt(Cb[:], Cb[:], eg3.h[:, nb_, 1:2], pdb.h[:, 0:129], ALU.mult, ALU.add, r=[Cb.b, eg3.b, pdb.b], w=[Cb.b])
                    cp(Cbf.h[:, nf + 1, 0:129], Cf[:], r=[Cf.b], w=[Cbf.bs[nf + 1]], eng="act")
                    cp(Cbb.h[:, nb_, 0:129], Cb[:], r=[Cb.b], w=[Cbb.bs[nb_]], eng="act")
            for i in range(NCH + 2):
                if i < NCH:
                    cs_ = slice(i * 64, (i + 1) * 64)
                    pq = PS[i % 2]
                    mm(pq.h[0:64, 0:64], kT.h[:, cs_], qT.h[:, cs_], True, True, r=[kT.b, qT.b], w=[pq.b])
                n = i - 1
                if 0 <= n < NCH:
                    cs_ = slice(n * 64, (n + 1) * 64)
                    pq = PS[n % 2]
                    s_ = sp_[n % 3]
                    stt(s_.h[:, 0, :], pq.h[0:64, 0:64], ew3.h[:, n, 0:1], MT_f, ALU.mult, ALU.mult,
                        r=[pq.b, ew3.b, cm.b], w=[s_.b])
                    stt(s_.h[:, 1, :], pq.h[0:64, 0:64], ew3.h[:, n, 1:2], MT_b, ALU.mult, ALU.mult,
                        r=[pq.b, ew3.b, cm.b], w=[s_.b])
                    po = PS[2 + (n % 2)]
                    mm(po.h[0:64, 0:129], s_.h[:, 0, :], va.h[:, n, 0:129], True, False, r=[s_.b, va.bs[n]], w=[po.b])
                    mm(po.h[0:64, 0:129], qT.h[:, cs_], Cbf.h[:, n, 0:129], False, True, r=[qT.b, Cbf.bs[n]], w=[po.b])
                    po2 = PS[4 + (n % 2)]
                    mm(po2.h[0:64, 0:129], s_.h[:, 1, :], va.h[:, n, 0:129], True, False, r=[s_.b, va.bs[n]], w=[po2.b])
                    mm(po2.h[0:64, 0:129], qT.h[:, cs_], Cbb.h[:, n + 1, 0:129], False, True,
                       r=[qT.b, Cbb.bs[n + 1]], w=[po2.b])
                n = i - 2
                if 0 <= n < NCH:
                    po = PS[2 + (n % 2)]
                    po2 = PS[4 + (n % 2)]
                    cp(stage.h[:, n, 0, 0:129], po.h[0:64, 0:129], r=[po.b], w=[stage.bs[n]], eng="act")
                    cp(stage.h[:, n, 1, 0:129], po2.h[0:64, 0:129], r=[po2.b], w=[stage.bs[n]])
            sb_all = stage.bs
            num0 = stage.h[:, :, 0, 0:128]
            num1 = stage.h[:, :, 1, 0:128]
            tt(den3[:], stage.h[:, :, :, 128], eb3[:], ALU.mult, r=sb_all + [eb3.b], w=[den3.b])
            ts(neg3[:], den3[:], -1.0, None, ALU.mult, None, r=[den3.b], w=[neg3.b])
            tt(den3[:], den3[:], neg3[:], ALU.max, r=[den3.b, neg3.b], w=[den3.b])
            ts(den3[:], den3[:], 1.0, None, ALU.max, None, r=[den3.b], w=[den3.b])
            recip(den3[:], den3[:], r=[den3.b], w=[den3.b])
            tt(den3[:], den3[:], eb3[:], ALU.mult, r=[den3.b, eb3.b], w=[den3.b])
            tt(num0, num0, den3.h[:, :, 0:1].broadcast_to([64, NCH, 128]), ALU.mult, r=sb_all + [den3.b], w=sb_all)
            tt(num1, num1, den3.h[:, :, 1:2].broadcast_to([64, NCH, 128]), ALU.mult, r=sb_all + [den3.b], w=sb_all,
               eng="pool")
            tt(num0, num0, num1, ALU.add, r=sb_all, w=sb_all)
            tt(num1, num0, num0, ALU.mult, r=sb_all, w=sb_all, eng="pool")
            S.op("dve", lambda e, o=ss3[:], i=num1: e.tensor_reduce(o, i, AXX, ALU.add), r=sb_all, w=[ss3.b])
            rstd_from_ss(ss3[:], ss3[:], 128, r=[ss3.b], w=[ss3.b])
            tt(num0, num0, ss3.h[:, :].unsqueeze(2).broadcast_to([64, NCH, 128]), ALU.mult, r=sb_all + [ss3.b], w=sb_all)
            tt(kh[:], num0, so[:], ALU.mult, r=sb_all + so.bs, w=kh.bs)
            for (n0, nn) in blocks_of(NCH, 8):
                pb_ = PB[(n0 // 8) % 2]
                for n in range(n0, n0 + nn):
                    tr(pb_.h[:, (n - n0) * 64:(n - n0 + 1) * 64], kh.h[:, n, :], ident.h[0:64, 0:64],
                       r=[kh.bs[n], ident.b], w=[pb_.b])
                cp(hmT.h[:, h, n0 * 64:(n0 + nn) * 64], pb_.h[:, 0:nn * 64], r=[pb_.b], w=[hmT.b], eng="act")
        S.barrier()
        A.release(m)

    def phase2(sl, nkt, KT, cnT, kb_src, prefetch=None):
        m = A.mark()
        wuq = A.alloc([128, 2, 768], BF16)
        wuqr = A.alloc([128, 2, 768], BF16)
        wukv = A.alloc([128, 1, 1024], BF16)
        cT = A.alloc([96, 2, L], F32)
        kbs = A.alloc([128, nkt], F32)
        Vh = A.alloc([128, nkt, 66], BF16)
        QT = [A.alloc([96, L], BF16) for _ in range(2)]
        attok = A.alloc([128, NT, 128], BF16, nb=NT)
        PT2 = [A.alloc([128, 2, 512], BF16) for _ in range(3)]
        t1 = A.alloc([96, 512], F32)
        t2 = A.alloc([96, 512], F32)
        rd = [A.alloc([128, 4], F32) for _ in range(2)]
        S.dma("sp", wuq[:], s_wuq, r=[bscr], w=[wuq.b])
        S.dma("sp", wuqr[:], s_wuqr, r=[bscr], w=[wuqr.b])
        S.dma("sp", wukv[:], s_wukv, r=[bscr], w=[wukv.b])
        S.dma("sp", cT.h[64:96, 0, :], csT[sl, 0], w=[cT.b])
        S.dma("sp", cT.h[64:96, 1, :], csT[sl, 1], w=[cT.b])
        S.dma("sp", kbs[:], kb_src, w=[kbs.b])
        if prefetch is not None:
            prefetch()
        memset(Vh[:], 1.0, w=[Vh.b])
        qblocks = blocks_of(L, 512)
        sctr = 0
        for h in range(8):
            for bix, (k0, kn) in enumerate(blocks_of(nkt * 128, 512)):
                tl = list(range(k0 // 128, (k0 + kn) // 128))
                pk_ = PS[bix % 4]
                mm(pk_.h[0:64, 0:kn], wukv.h[:, 0, h * 128:h * 128 + 64], cnT.h[:, k0:k0 + kn], True, True,
                   r=[wukv.b] + [cnT.bs[i] for i in tl], w=[pk_.b])
                cp(KT.h[0:64, k0:k0 + kn], pk_.h[0:64, 0:kn], r=[pk_.b], w=[KT.bs[i] for i in tl],
                   eng=("act" if bix % 2 else "dve"))
            for bix, (j0, jn) in enumerate(blocks_of(nkt, 8)):
                pv_ = PS[bix % 4]
                for j in range(j0, j0 + jn):
                    mm(pv_.h[:, (j - j0) * 64:(j - j0 + 1) * 64], cnT.h[:, j * 128:(j + 1) * 128],
                       wukv.h[:, 0, h * 128 + 64:h * 128 + 128], True, True, r=[cnT.bs[j], wukv.b], w=[pv_.b])
                cp(Vh.h[:, j0:j0 + jn, 0:64], pv_.h[:, 0:jn * 64].rearrange("p (j c) -> p j c", c=64),
                   r=[pv_.b], w=[Vh.b], eng=("dve" if bix % 2 else "act"))
            Q = QT[h % 2]
            for (q0, qn_) in qblocks:
                tl = list(range(q0 // 128, (q0 + qn_) // 128))
                for kt in range(2):
                    mm(PS[2].h[0:96, 0:qn_], wuq.h[:, kt, h * 96:(h + 1) * 96], qnT.h[:, kt, q0:q0 + qn_],
                       kt == 0, kt == 1, r=[wuq.b] + [qnT.bs[i] for i in tl], w=[PS[2].b])
                for kt in range(2):
                    mm(PS[3].h[0:96, 0:qn_], wuqr.h[:, kt, h * 96:(h + 1) * 96], qnT.h[:, kt, q0:q0 + qn_],
                       kt == 0, kt == 1, r=[wuqr.b] + [qnT.bs[i] for i in tl], w=[PS[3].b])
                cp(Q.h[0:64, q0:q0 + qn_], PS[2].h[0:64, 0:qn_], r=[PS[2].b], w=[Q.b], eng="act")
                tt(t1.h[64:96, 0:qn_], PS[2].h[64:96, 0:qn_], cT.h[64:96, 0, q0:q0 + qn_], ALU.mult,
                   r=[PS[2].b, cT.b], w=[t1.b])
                tt(t2.h[64:96, 0:qn_], PS[3].h[64:96, 0:qn_], cT.h[64:96, 1, q0:q0 + qn_], ALU.mult,
                   r=[PS[3].b, cT.b], w=[t2.b])
                tt(Q.h[64:96, q0:q0 + qn_], t1.h[64:96, 0:qn_], t2.h[64:96, 0:qn_], ALU.add,
                   r=[t1.b, t2.b], w=[Q.b])
            groups = [[0]]
            k_ = 1
            while k_ < nkt - 1:
                if k_ + 1 < nkt - 1:
                    groups.append([k_, k_ + 1])
                    k_ += 2
                else:
                    groups.append([k_])
                    k_ += 1
            if nkt > 1:
                groups.append([nkt - 1])
            masked = {0, nkt - 1}
            its = [(bi, q0, qn_, g) for bi, (q0, qn_) in enumerate(qblocks) for g in groups]
            LOOK = 1

            def emit_s(i):
                bi, q0, qn_, g = its[i]
                k2 = i % 2
                pt_ = PT2[i % 3]
                bufs = [PS[2 * k2].b, PS[2 * k2 + 1].b]
                for j_, kt in enumerate(g):
                    sb_ = PS[2 * k2 + j_]
                    mm(sb_.h[:, 0:qn_], KT.h[0:96, kt * 128:(kt + 1) * 128], Q.h[0:96, q0:q0 + qn_], True, True,
                       r=[KT.bs[kt], Q.b], w=[sb_.b])
                if len(g) == 1:
                    kt = g[0]
                    if kt in masked:
                        act(pt_.h[:, 0, 0:qn_], PS[2 * k2].h[:, 0:qn_], AF.Exp, r=[PS[2 * k2].b, kbs.b], w=[pt_.b],
                            bias=kbs.h[:, kt:kt + 1], scale=SCALE_A)
                    else:
                        act(pt_.h[:, 0, 0:qn_], PS[2 * k2].h[:, 0:qn_], AF.Exp, r=[PS[2 * k2].b], w=[pt_.b],
                            scale=SCALE_A)
                else:
                    src = P2[k2][:, :].rearrange("p (b c) -> p b c", b=2)[:, :, 0:qn_]
                    act(pt_.h[:, :, 0:qn_], src, AF.Exp, r=bufs, w=[pt_.b], scale=SCALE_A)

            def emit_pv(i):
                bi, q0, qn_, g = its[i]
                nsub = qn_ // 128
                acc = PS[4 + (bi % 2)]
                pt_ = PT2[i % 3]
                for j_, kt in enumerate(g):
                    for s in range(nsub):
                        mm(acc.h[:, s * 65:(s + 1) * 65], pt_.h[:, j_, s * 128:(s + 1) * 128], Vh.h[:, kt, 0:65],
                           (kt == 0 and s == 0), kt == nkt - 1, r=[pt_.b, Vh.b], w=[acc.b], skip=True)
                if g[-1] == nkt - 1:
                    r_ = rd[bi % 2]
                    a3 = acc.h[:, 0:nsub * 65].rearrange("p (s c) -> p s c", c=65)
                    recip(r_.h[:, 0:nsub], a3[:, :, 64], r=[acc.b], w=[r_.b])
                    for s in range(nsub):
                        ti = q0 // 128 + s
                        ts(attok.h[:, ti, (h % 2) * 64:(h % 2) * 64 + 64], a3[:, s, 0:64], r_.h[:, s:s + 1], None,
                           ALU.mult, None, r=[acc.b, r_.b], w=[attok.bs[ti]])

            for i in range(min(LOOK, len(its))):
                emit_s(i)
            for i in range(len(its)):
                if i + LOOK < len(its):
                    emit_s(i + LOOK)
                emit_pv(i)
            if h % 2 == 1:
                for (i0, in_) in blocks_of(NT, 4):
                    for i in range(i0, i0 + in_):
                        tr(psb(0, (i - i0) * 128, 128), attok.h[:, i, :], ident[:], r=[attok.bs[i], ident.b], w=[PB[0].b])
                    cp(attnT.h[:, h // 2, i0 * 128:(i0 + in_) * 128], psb(0, 0, in_ * 128), r=[PB[0].b], w=[attnT.b])
        S.barrier()
        A.release(m)

    def alloc_p3a_weights():
        return (A.alloc([128, 8, 2048], BF16), A.alloc([128, 4, D], BF16), A.alloc([128, 4, D], BF16),
                A.alloc([128, 8, D], BF16))

    def load_p3a_weights(ws):
        wg, woa, wom, wo = ws
        S.dma("sp", wg[:], s_win[:, :, 2480:4528], r=[bscr], w=[wg.b])
        S.dma("sp", woa[:], s_woa, r=[bscr], w=[woa.b])
        S.dma("sp", wom[:], s_wom, r=[bscr], w=[wom.b])
        S.dma("sp", wo[:], s_wout, r=[bscr], w=[wo.b])

    def phase3a(sl, pre=None):
        m = A.mark()
        if pre is None:
            pre = alloc_p3a_weights()
            load_p3a_weights(pre)
        wg, woa, wom, wo = pre
        ubf = [A.alloc([128, D], BF16) for _ in range(2)]
        junk = A.alloc([128, D], BF16)
        ssb = [A.alloc([128, 2], F32) for _ in range(2)]
        rsb = [A.alloc([128, 2], F32) for _ in range(2)]
        ut = [A.alloc([128, 8, 128], BF16) for _ in range(2)]
        sg = [A.alloc([128, 512], F32) for _ in range(2)]
        yT = A.alloc([128, 8, 128], BF16)
        h2 = [A.alloc([128, D], F32) for _ in range(2)]
        v128 = A.alloc([128, NT], F32)
        S.dma("sp", v128[:], valid128[sl], w=[v128.b])
        xb3 = [A.alloc([128, D], F32) for _ in range(3)]
        ybf2 = [A.alloc([128, D], BF16) for _ in range(2)]
        ubf2 = [A.alloc([128, D], BF16) for _ in range(2)]
        ss2 = [A.alloc([128, 2], F32) for _ in range(2)]
        rs2 = [A.alloc([128, 2], F32) for _ in range(2)]

        def sA(i):
            p = i % 2
            tsl = slice(i * 128, (i + 1) * 128)
            front_tile(xs[sl, tsl, :], xb3[i % 3], ubf[p], ssb[p], rsb[p], junk, 0, ut[p][:], [ut[p].b])

        def sB(i):
            p = i % 2
            tsl = slice(i * 128, (i + 1) * 128)
            u = ut[p]
            yb = ybf2[p]
            for half in range(2):
                c0 = half * 512
                for kt in range(8):
                    mm(PS[1].h[:, :], u.h[:, kt, :], wg.h[:, kt, c0:c0 + 512], kt == 0, kt == 7, r=[u.b, wg.b], w=[PS[1].b])
                for kt in range(4):
                    mm(PS[2].h[:, :], attnT.h[:, kt, tsl], woa.h[:, kt, c0:c0 + 512], kt == 0, kt == 3,
                       r=[attnT.b, woa.b], w=[PS[2].b])
                for kt in range(8):
                    mm(PS[3].h[:, :], u.h[:, kt, :], wg.h[:, kt, 1024 + c0:1024 + c0 + 512], kt == 0, kt == 7,
                       r=[u.b, wg.b], w=[PS[3].b])
                for kt in range(4):
                    mm(PS[4].h[:, :], hmT.h[:, kt, tsl], wom.h[:, kt, c0:c0 + 512], kt == 0, kt == 3,
                       r=[hmT.b, wom.b], w=[PS[4].b])
                act(sg[0][:], PS[1].h[:, :], AF.Sigmoid, r=[PS[1].b], w=[sg[0].b])
                act(sg[1][:], PS[3].h[:, :], AF.Sigmoid, r=[PS[3].b], w=[sg[1].b])
                tt(sg[0][:], sg[0][:], PS[2].h[:, :], ALU.mult, r=[sg[0].b, PS[2].b], w=[sg[0].b])
                tt(sg[1][:], sg[1][:], PS[4].h[:, :], ALU.mult, r=[sg[1].b, PS[4].b], w=[sg[1].b])
                tt(yb.h[:, c0:c0 + 512], sg[0][:], sg[1][:], ALU.add, r=[sg[0].b, sg[1].b], w=[yb.b])

        def sC(i):
            p = i % 2
            tsl = slice(i * 128, (i + 1) * 128)
            yb = ybf2[p]
            x_ = xb3[i % 3]
            for kt in range(8):
                tr(psb(1, kt * 128, 128), yb.h[:, kt * 128:(kt + 1) * 128], ident[:], r=[yb.b, ident.b], w=[PB[1].b])
            cp(yT[:], psb(1, 0, 1024).rearrange("p (k t) -> p k t", k=8), r=[PB[1].b], w=[yT.b], eng="act")
            hh_ = h2[p]
            for half in range(2):
                c0 = half * 512
                pb = PS[5] if half == 0 else PS[0]
                for kt in range(8):
                    mm(pb.h[:, :], yT.h[:, kt, :], wo.h[:, kt, c0:c0 + 512], kt == 0, kt == 7, r=[yT.b, wo.b], w=[pb.b])
                tt(hh_.h[:, c0:c0 + 512], pb.h[:, :], x_.h[:, c0:c0 + 512], ALU.add, r=[pb.b, x_.b], w=[hh_.b])
            S.dma("pool", s_h2[tsl, :], hh_[:], r=[hh_.b], w=[bh2[i]])
            act(junk[:], hh_[:], AF.Square, r=[hh_.b], w=[junk.b, ss2[p].b], accum=ss2[p].h[:, 0:1])
            rstd_from_ss(rs2[p].h[:, 0:1], ss2[p].h[:, 0:1], D, r=[ss2[p].b], w=[rs2[p].b])
            tt(rs2[p].h[:, 0:1], rs2[p].h[:, 0:1], v128.h[:, i:i + 1], ALU.mult, r=[rs2[p].b, v128.b], w=[rs2[p].b])
            ts(ubf2[p][:], hh_[:], rs2[p].h[:, 0:1], None, ALU.mult, None, r=[hh_.b, rs2[p].b], w=[ubf2[p].b])

        def sD(i):
            p = i % 2
            tsl = slice(i * 128, (i + 1) * 128)
            for kt in range(8):
                tr(psb(0, kt * 128, 128), ubf2[p].h[:, kt * 128:(kt + 1) * 128], ident[:], r=[ubf2[p].b, ident.b], w=[PB[0].b])
            cp(uT.h[:, :, tsl], psb(0, 0, 1024).rearrange("p (k t) -> p k t", k=8), r=[PB[0].b], w=[uT.bs[i]], eng="act")

        for it in range(NT + 3):
            if 0 <= it - 3 < NT:
                sD(it - 3)
            if 0 <= it - 2 < NT:
                sC(it - 2)
            if 0 <= it - 1 < NT:
                sB(it - 1)
            if it < NT:
                sA(it)
        S.barrier()
        A.release(m)

    def phase3b(sl):
        m = A.mark()
        G = 2
        NG = NFT // G
        wu = [A.alloc([128, G, 8, 128], BF16) for _ in range(2)]
        wgt = [A.alloc([128, G, 8, 128], BF16) for _ in range(2)]
        hid = A.alloc([128, NFT, 512], BF16, nb=NFT)
        ab = [A.alloc([128, 516], F32) for _ in range(2)]
        cb_ = [A.alloc([128, 512], F32) for _ in range(2)]
        gb_ = [A.alloc([128, 512], F32) for _ in range(2)]
        h2t = [A.alloc([128, D], F32) for _ in range(4)]
        ot = [A.alloc([128, D], F32) for _ in range(2)]
        junk = A.alloc([128, D], BF16)
        ss = [A.alloc([128, 2], F32) for _ in range(2)]
        octr = 0
        for (b0, bn) in blocks_of(NOWNT, 512):
            t0 = 64 + b0
            tl = sorted(set([(t0 - 1) // 128] + list(range(t0 // 128, (t0 + bn) // 128 + 1))))
            tl = [i for i in tl if i < NT]
            nsubs = (bn + 127) // 128
            for s in range(nsubs):
                r0 = t0 + s * 128
                rn = min(128, bn - s * 128)
                rtl = sorted(set([r0 // 128, (r0 + rn - 1) // 128]))
                S.dma("sp", h2t[s].h[0:rn, :], s_h2[r0:r0 + rn, :], r=[bh2[i] for i in rtl], w=[h2t[s].b])
            for g in range(NG):
                p = g % 2
                S.dma("sp", wu[p][:], s_wup[g * G:(g + 1) * G].rearrange("f p k c -> p f k c"), r=[bscr], w=[wu[p].b])
                S.dma("sp", wgt[p][:], s_wgate[g * G:(g + 1) * G].rearrange("f p k c -> p f k c"), r=[bscr], w=[wgt[p].b])
                for fi in range(G):
                    f = g * G + fi
                    q = f % 2
                    pa = PS[0 + q]
                    pa2 = PS[2 + q]
                    pg = PS[4 + q]
                    ub = [uT.bs[i] for i in tl]
                    for kt in range(8):
                        mm(pa.h[:, 0:bn], wu[p].h[:, fi, kt, :], uT.h[:, kt, t0 - 1:t0 - 1 + bn], kt == 0, kt == 7,
                           r=[wu[p].b] + ub, w=[pa.b])
                    for kt in range(8):
                        mm(pa2.h[:, 0:2], wu[p].h[:, fi, kt, :], uT.h[:, kt, t0 - 1 + bn:t0 + 1 + bn], kt == 0, kt == 7,
                           r=[wu[p].b] + ub, w=[pa2.b])
                    for kt in range(8):
                        mm(pg.h[:, 0:bn], wgt[p].h[:, fi, kt, :], uT.h[:, kt, t0:t0 + bn], kt == 0, kt == 7,
                           r=[wgt[p].b] + ub, w=[pg.b])
                    a_ = ab[q]
                    cp(a_.h[:, 0:bn], pa.h[:, 0:bn], r=[pa.b], w=[a_.b], eng="act")
                    cp(a_.h[:, bn:bn + 2], pa2.h[:, 0:2], r=[pa2.b], w=[a_.b], eng="act")
                    c_ = cb_[q]
                    ts(c_.h[:, 0:bn], a_.h[:, 0:bn], cvp.h[:, f, 0:1], None, ALU.mult, None, r=[a_.b, cvp.b], w=[c_.b])
                    stt(c_.h[:, 0:bn], a_.h[:, 1:bn + 1], cvp.h[:, f, 1:2], c_.h[:, 0:bn], ALU.mult, ALU.add,
                        r=[a_.b, cvp.b, c_.b], w=[c_.b])
                    stt(c_.h[:, 0:bn], a_.h[:, 2:bn + 2], cvp.h[:, f, 2:3], c_.h[:, 0:bn], ALU.mult, ALU.add,
                        r=[a_.b, cvp.b, c_.b], w=[c_.b])
                    g_ = gb_[q]
                    act(g_.h[:, 0:bn], c_.h[:, 0:bn], AF.Gelu_apprx_tanh, r=[c_.b, cvp.b], w=[g_.b], bias=cvp.h[:, f, 3:4])
                    tt(hid.h[:, f, 0:bn], g_.h[:, 0:bn], pg.h[:, 0:bn], ALU.mult, r=[g_.b, pg.b], w=[hid.bs[f]])
            for s in range(nsubs):
                rn = min(128, bn - s * 128)
                o_ = ot[octr % 2]
                sq = ss[octr % 2]
                octr += 1
                for half in range(2):
                    c0 = half * 512
                    pb = PS[half]
                    for f in range(NFT):
                        mm(pb.h[0:rn, :], hid.h[:, f, s * 128:s * 128 + rn], wdfull.h[:, f, c0:c0 + 512], f == 0, f == NFT - 1,
                           r=[hid.bs[f], wdfull.b], w=[pb.b])
                    tt(o_.h[0:rn, c0:c0 + 512], pb.h[0:rn, :], h2t[s].h[0:rn, c0:c0 + 512], ALU.add,
                       r=[pb.b, h2t[s].b], w=[o_.b])
                act(junk.h[0:rn, :], o_.h[0:rn, :], AF.Square, r=[o_.b], w=[junk.b, sq.b], accum=sq.h[0:rn, 0:1])
                rstd_from_ss(sq.h[0:rn, 1:2], sq.h[0:rn, 0:1], D, r=[sq.b], w=[sq.b])
                stt(o_.h[0:rn, :], o_.h[0:rn, :], sq.h[0:rn, 1:2], gfin.h[0:rn, :], ALU.mult, ALU.mult,
                    r=[o_.b, sq.b, gfin.b], w=[o_.b])
                S.dma("pool", y[sl, b0 + s * 128:b0 + s * 128 + rn, :], o_.h[0:rn, :], r=[o_.b], w=[])
        S.barrier()
        A.release(m)

    wdfull = None

    def phase3b_wrapped(sl):
        nonlocal wdfull
        m = A.mark()
        wdfull = A.alloc([128, NFT, D], BF16)
        S.dma("sp", wdfull[:], s_wdown, r=[bscr], w=[wdfull.b])
        phase3b(sl)
        A.release(m)

    import os
    STOP = int(os.environ.get("KSTOP", "99"))
    for sl in range(NSL):
        if STOP == 0:
            break
        sample = (sl == NPS)
        nkt = NKF if sample else NT
        A.release(m_slice)
        qnT = A.alloc([128, 2, L], BF16, nb=NT)
        hmT = A.alloc([128, 4, L], BF16, nb=1)
        KTp = cnTp = None
        if sample:
            scan_pass()
        else:
            KTp = A.alloc([96, NT * 128], BF16, nb=NT)
            cnTp = A.alloc([128, NT * 128], BF16, nb=NT)
        phase0(sl, None if sample else (KTp, cnTp))
        if STOP == 1:
            break
        phase1(sl, sample)
        if STOP == 2:
            break
        attnT = A.alloc([128, 4, L], BF16, nb=1)
        pre3 = None if sample else alloc_p3a_weights()
        m = A.mark()
        A.relaxed = True
        if sample:
            KT = A.alloc([96, nkt * 128], BF16, nb=nkt)
            cnT = A.alloc([128, nkt * 128], BF16, nb=nkt)
            kv_pass(lambda j: xfull[j * 128:(j + 1) * 128, :], nkt, cs_full, KT, cnT)
            phase2(sl, nkt, KT, cnT, kbias_full)
        else:
            phase2(sl, nkt, KTp, cnTp, kbias_sl[sl], prefetch=lambda: load_p3a_weights(pre3))
        A.release(m)
        A.relaxed = False
        if STOP == 3:
            break
        phase3a(sl, pre3)
        if STOP == 4:
            break
        A.release(m_slice)
        phase3b_wrapped(sl)
        if STOP == 5:
            break
    S.barrier(engines=("sp",))
    S.emit()
    return nc, S


def _rope_tables(npos):
    inv = (10000.0 ** (-np.arange(0, 32, 2, dtype=np.float32) / np.float32(32))).astype(np.float32)
    ang = np.arange(npos, dtype=np.float32)[:, None] * inv[None, :]
    return np.cos(ang).astype(np.float32), np.sin(ang).astype(np.float32)


def make_inputs(NPS, NOWN, NKF, S_full, x_prompt, x_sample, meta_tokens, norm_mix, w_in, q_norm, w_uq, kv_norm,
                w_ukv, b_igate, b_fgate, mlstm_norm, w_o_attn, w_o_mlstm, w_out, norm_ffn, w_up, w_gate,
                conv_w, conv_b, w_down, norm_final):
    NSL = NPS + 1
    NCH = NOWN + 2
    L = 64 * NCH
    NT = L // 128
    f32 = np.float32
    meta = np.asarray(meta_tokens, f32)
    SEQ = x_prompt.shape[1]
    assert SEQ == 64 * NOWN and S_full == 8 * 64 * NOWN
    TF = 64 + S_full
    PF = NKF * 128
    cos_all, sin_all = _rope_tables(TF)

    def cs_for_positions(pos):
        rp = np.clip(pos - 48, 0, TF - 1)
        return np.concatenate([cos_all[rp], sin_all[rp]], axis=1).astype(f32)

    xfull = np.zeros((PF, D), f32)
    xfull[48:64] = meta
    xfull[64:64 + S_full] = np.asarray(x_sample, f32)[0]
    posfull = np.arange(PF)
    validfull_v = ((posfull >= 48) & (posfull < TF)).astype(f32)
    validfull = np.ascontiguousarray(validfull_v.reshape(NKF, 128).T)
    kbias_full = np.ascontiguousarray(((1.0 - validfull_v) * KBIAS_NEG).astype(f32).reshape(NKF, 128).T)
    cs_full = np.ascontiguousarray(cs_for_positions(posfull).reshape(NKF, 128, 32).transpose(1, 0, 2))

    def rep(v, n=128):
        return np.ascontiguousarray(np.broadcast_to(np.asarray(v, f32).reshape(1, -1), (n, np.asarray(v).size)))

    def pk(v, kt):
        return np.ascontiguousarray(np.asarray(v, f32).reshape(kt, 128).T)

    cmat = np.zeros((128, 5, 128), f32)
    s_idx = np.arange(128)[:, None]
    t_idx = np.arange(128)[None, :]
    cmat[:64, 0, :64] = (s_idx[:64, :] <= t_idx[:, :64])
    cmat[:64, 1, :64] = (s_idx[:64, :] >= t_idx[:, :64])
    cmat[:64, 2, :64] = (s_idx[:64, :] <= t_idx[:, :64])
    cmat[:64, 3, :64] = (s_idx[:64, :] >= t_idx[:, :64])
    cmat[:, 4, :] = (s_idx <= t_idx)
    convp = np.zeros((128, NFT, 4), f32)
    cw = np.asarray(conv_w, f32)[0]
    cbias = np.asarray(conv_b, f32)[0]
    for j in range(3):
        convp[:, :, j] = cw[j].reshape(NFT, 128).T
    convp[:, :, 3] = cbias.reshape(NFT, 128).T
    bgate = rep(np.concatenate([np.asarray(b_igate, f32).reshape(-1), np.asarray(b_fgate, f32).reshape(-1)]))
    common = dict(
        xfull=xfull, w_in=np.ascontiguousarray(np.asarray(w_in, f32)[0]), w_uq=np.ascontiguousarray(np.asarray(w_uq, f32)[0]),
        w_ukv=np.ascontiguousarray(np.asarray(w_ukv, f32)[0]), w_oa=np.ascontiguousarray(np.asarray(w_o_attn, f32)[0]),
        w_om=np.ascontiguousarray(np.asarray(w_o_mlstm, f32)[0]), w_out=np.ascontiguousarray(np.asarray(w_out, f32)[0]),
        w_up=np.ascontiguousarray(np.asarray(w_up, f32)[0]), w_gate=np.ascontiguousarray(np.asarray(w_gate, f32)[0]),
        w_down=np.ascontiguousarray(np.asarray(w_down, f32)[0]),
        g_mix=pk(np.asarray(norm_mix)[0], 8), g_q=pk(np.asarray(q_norm)[0], 2), g_kv=pk(np.asarray(kv_norm)[0], 1),
        g_ml=pk(np.asarray(mlstm_norm)[0], 4), g_ffn=pk(np.asarray(norm_ffn)[0], 8), g_fin=rep(norm_final),
        bgate=bgate, convp=convp, kbias_full=kbias_full, validfull=validfull, cs_full=cs_full, cmat=cmat,
        identsrc=np.eye(128, dtype=f32),
    )
    xp = np.asarray(x_prompt, f32)
    in_maps = []
    for c in range(NCORES):
        xs = np.zeros((NSL, L, D), f32)
        valid = np.zeros((NSL, L), f32)
        pos = np.zeros((NSL, L), np.int64)
        for i in range(NPS):
            xs[i, 48:64] = meta
            xs[i, 64:64 + SEQ] = xp[c * NPS + i]
            valid[i, 48:64 + SEQ] = 1.0
            pos[i] = np.arange(L)
        g0 = 64 * NOWN * c
        gp = g0 + np.arange(L)
        ok = gp < PF
        xs[NPS, ok] = xfull[gp[ok]]
        valid[NPS] = ((gp >= 48) & (gp < TF)).astype(f32)
        pos[NPS] = gp
        cs = np.stack([cs_for_positions(pos[i]) for i in range(NSL)])
        cs_tok = np.ascontiguousarray(cs.reshape(NSL, NT, 128, 32).transpose(0, 2, 1, 3))
        csT = np.zeros((NSL, 2, 32, L), f32)
        csT[:, 0, 0:16] = cs[:, :, 0:16].transpose(0, 2, 1)
        csT[:, 0, 16:32] = cs[:, :, 0:16].transpose(0, 2, 1)
        csT[:, 1, 0:16] = cs[:, :, 16:32].transpose(0, 2, 1)
        csT[:, 1, 16:32] = cs[:, :, 16:32].transpose(0, 2, 1)
        valid128 = np.ascontiguousarray(valid.reshape(NSL, NT, 128).transpose(0, 2, 1))
        valid64 = np.ascontiguousarray(valid.reshape(NSL, NCH, 64).transpose(0, 2, 1))
        kbias_sl = np.ascontiguousarray(((1.0 - valid[:NPS]) * KBIAS_NEG).astype(f32).reshape(NPS, NT, 128).transpose(0, 2, 1))
        tj = np.arange(NKF)
        jl = (64 * NOWN * c) // 128
        actf = (tj < jl).astype(f32)
        actb = (tj >= jl + NT).astype(f32)
        a8 = np.zeros((128, NKF, 8), f32)
        a8[:, :, 0:4] = actf[None, :, None]
        a8[:, :, 4:8] = actb[None, :, None]
        d = dict(common)
        d.update(xs=xs, valid128=valid128, valid64=valid64, kbias_sl=kbias_sl, act8=a8, cs_tok=cs_tok, csT=csT)
        in_maps.append(d)
    return in_maps


_CACHE = {}


def run(NPS, NOWN, NKF, S_full, inputs, trace=False):
    key = (NPS, NOWN, NKF)
    if key not in _CACHE:
        _CACHE[key] = build_program(NPS, NOWN, NKF)
    nc, S = _CACHE[key]
    in_maps = make_inputs(NPS, NOWN, NKF, S_full, **inputs)
    res = run_bass_kernel_spmd(nc, in_maps, core_ids=list(range(NCORES)))
    ys = [np.asarray(r["y"]) for r in res.results]
    NOWNT = 64 * NOWN
    y_prompt = np.concatenate([ys[c][:NPS] for c in range(NCORES)], axis=0).astype(np.float32)
    y_sample = np.concatenate([ys[c][NPS] for c in range(NCORES)], axis=0)[None].astype(np.float32)
    return y_prompt, y_sample


def kernel(**inputs):
    x_prompt = np.asarray(inputs["x_prompt"])
    x_sample = np.asarray(inputs["x_sample"])
    B, SEQ, _ = x_prompt.shape
    S_full = x_sample.shape[1]
    NPS = B // NCORES
    NOWN = SEQ // 64
    NKF = (64 + S_full + 127) // 128
    return run(NPS, NOWN, NKF, S_full, inputs)
```

```python
import numpy as np
import concourse.bass as bass
import concourse.mybir as mybir
from concourse.bass_utils import run_bass_kernel_spmd

F32 = mybir.dt.float32
BF16 = mybir.dt.bfloat16
AF = mybir.ActivationFunctionType
ALU = mybir.AluOpType

D = 1024
NMETA = 16
EPS = 1e-6
DFF = 2816
NFT = DFF // 128
NCORES = 8
NDS = 8
SCALE_A = 96 ** -0.5
SCALE_K = 128 ** -0.5
KBIAS_NEG = -30000.0
STRICT_SAME_ENGINE = True


class Buf:
    __slots__ = ("w", "r", "excl")

    def __init__(self):
        self.w = None
        self.r = {}
        self.excl = False


class T:
    def __init__(self, h, nb=1):
        self.h = h
        self.bs = [Buf() for _ in range(nb)]

    @property
    def b(self):
        return self.bs[0]

    def __getitem__(self, k):
        return self.h[k]


class Sched:
    ENG = ("pe", "act", "dve", "pool", "sp")

    def __init__(self, nc):
        self.nc = nc
        self.eng = {}
        for name in self.ENG:
            self.eng[name] = dict(sem=nc.alloc_semaphore("s_" + name), count=0, recs=[], seen={})
        self.dsem = {}
        for q in ("sp", "pool", "act"):
            self.dsem[q] = [[nc.alloc_semaphore("d_%s%d" % (q, i)), 0] for i in range(NDS)]
        self.drr = {"sp": 0, "pool": 0, "act": 0}
        self.snaps = {}
        self.nops = 0

    def semof(self, key):
        if isinstance(key, str):
            return self.eng[key]["sem"]
        return self.dsem[key[1]][key[2]][0]

    def _need(self, E, tok, waits):
        key, val = tok
        seen = self.eng[E]["seen"]
        if seen.get(key, 0) >= val:
            return
        if key == E and E == "pe":
            return
        waits.append(tok)
        seen[key] = val
        snap = self.snaps.get(tok)
        if snap:
            for k, v in snap.items():
                if seen.get(k, 0) < v:
                    seen[k] = v

    def _deps(self, E, r, w):
        waits = []
        for b in r:
            if b.w is not None:
                self._need(E, b.w, waits)
        for b in w:
            if b.w is not None and (STRICT_SAME_ENGINE or b.w[0] != E):
                self._need(E, b.w, waits)
            for k, v in b.r.items():
                if STRICT_SAME_ENGINE or k != E:
                    self._need(E, (k, v), waits)
        return waits

    def op(self, E, fn, r=(), w=()):
        e = self.eng[E]
        if any(b.excl for b in r):
            w = list(w) + [b for b in r if b.excl and b not in w]
            r = [b for b in r if not b.excl]
        waits = self._deps(E, r, w)
        e["count"] += 1
        tok = (E, e["count"])
        self.snaps[tok] = dict(e["seen"])
        for b in r:
            if b.r.get(E, 0) < tok[1]:
                b.r[E] = tok[1]
        for b in w:
            b.w = tok
            b.r = {}
        e["recs"].append((waits, fn, (e["sem"], 1)))
        self.nops += 1
        return tok

    def dma(self, q, out_ap, in_ap, r=(), w=()):
        e = self.eng[q]
        waits = self._deps(q, r, w)
        slot = self.drr[q]
        self.drr[q] = (slot + 1) % NDS
        rec = self.dsem[q][slot]
        key = ("d", q, slot)
        if rec[1] > 0:
            self._need(q, (key, rec[1]), waits)
        rec[1] += 16
        tok = (key, rec[1])
        self.snaps[tok] = dict(e["seen"])
        for b in r:
            if b.r.get(key, 0) < tok[1]:
                b.r[key] = tok[1]
        for b in w:
            b.w = tok
            b.r = {}
        e["recs"].append((waits, (lambda eng, o=out_ap, i=in_ap: eng.dma_start(out=o, in_=i)), (rec[0], 16)))
        return tok

    def all_tokens(self):
        toks = []
        for name in ("pe", "act", "dve", "pool"):
            c = self.eng[name]["count"]
            if c > 0:
                toks.append((name, c))
        for q in self.dsem:
            for i, rec in enumerate(self.dsem[q]):
                if rec[1] > 0:
                    toks.append((("d", q, i), rec[1]))
        return toks

    def barrier(self, engines=None):
        toks = self.all_tokens()
        for E in (engines or self.ENG):
            waits = []
            for tok in toks:
                if tok[0] == E:
                    continue
                self._need(E, tok, waits)
            if waits:
                self.eng[E]["recs"].append((waits, None, None))

    def emit(self):
        def run(name):
            def body(eng):
                for waits, fn, inc in self.eng[name]["recs"]:
                    for key, val in waits:
                        eng.wait_ge(self.semof(key), val)
                    if fn is not None:
                        ins = fn(eng)
                        ins.then_inc(inc[0], inc[1])
            return body

        with self.nc.Block() as block:
            block.sync(run("sp"))
            block.tensor(run("pe"))
            block.scalar(run("act"))
            block.vector(run("dve"))
            block.gpsimd(run("pool"))


class Arena:
    def __init__(self, nc):
        self.nc = nc
        self.base = (nc.sbuf_base + 63) // 64 * 64
        self.top = nc.sbuf_top
        self.cur = self.base
        self.n = 0
        self.limit = self.top
        self.relaxed = False
        self.peak = 0

    def alloc_top(self, shape, dtype, nb=1):
        esz = 2 if dtype == BF16 else 4
        n = esz
        for s_ in shape[1:]:
            n *= s_
        off = (self.limit - n) // 64 * 64
        self.limit = off
        self.n += 1
        h = self.nc.alloc_sbuf_tensor_at("t%d" % self.n, list(shape), dtype, offset=off)
        return T(h, nb)

    def mark(self):
        return self.cur

    def release(self, m):
        self.cur = m

    def alloc(self, shape, dtype, nb=1):
        esz = 2 if dtype == BF16 else 4
        n = esz
        for s in shape[1:]:
            n *= s
        off = self.cur
        self.cur = (off + n + 63) // 64 * 64
        lim = self.top if self.relaxed else self.limit
        assert self.cur <= lim, "SBUF overflow %d > %d" % (self.cur, lim)
        self.peak = max(self.peak, self.cur)
        self.n += 1
        h = self.nc.alloc_sbuf_tensor_at("t%d" % self.n, list(shape), dtype, offset=off)
        return T(h, nb)


def blocks_of(total, bs):
    out = []
    o = 0
    while o < total:
        out.append((o, min(bs, total - o)))
        o += bs
    return out


def build_program(NPS, NOWN, NKF):
    NSL = NPS + 1
    NCH = NOWN + 2
    L = 64 * NCH
    NT = L // 128
    NOWNT = 64 * NOWN
    nc = bass.Bass("TRN2", target_bir_lowering=False)

    def din(name, shape, dt=F32):
        return nc.dram_tensor(name, list(shape), dt, kind="ExternalInput").ap()

    xs = din("xs", [NSL, L, D])
    xfull = din("xfull", [NKF * 128, D])
    w_in = din("w_in", [D, 4528])
    w_uq = din("w_uq", [256, 768])
    w_ukv = din("w_ukv", [128, 1024])
    w_oa = din("w_oa", [512, D])
    w_om = din("w_om", [512, D])
    w_out = din("w_out", [D, D])
    w_up = din("w_up", [D, DFF])
    w_gate = din("w_gate", [D, DFF])
    w_down = din("w_down", [DFF, D])
    g_mix = din("g_mix", [128, 8])
    g_q = din("g_q", [128, 2])
    g_kv = din("g_kv", [128, 1])
    g_ml = din("g_ml", [128, 4])
    g_ffn = din("g_ffn", [128, 8])
    g_fin = din("g_fin", [128, D])
    bgate = din("bgate", [128, 16])
    convp = din("convp", [128, NFT, 4])
    valid128 = din("valid128", [NSL, 128, NT])
    valid64 = din("valid64", [NSL, 64, NCH])
    kbias_sl = din("kbias_sl", [NPS, 128, NT])
    kbias_full = din("kbias_full", [128, NKF])
    validfull = din("validfull", [128, NKF])
    act8 = din("act8", [128, NKF, 8])
    cs_tok = din("cs_tok", [NSL, 128, NT, 32])
    cs_full = din("cs_full", [128, NKF, 32])
    csT = din("csT", [NSL, 2, 32, L])
    cmat = din("cmat", [128, 5, 128])
    y = nc.dram_tensor("y", [NSL, NOWNT, D], F32, kind="ExternalOutput").ap()

    def dscr(name, shape, dt=BF16):
        return nc.dram_tensor(name, list(shape), dt).ap()

    s_win = dscr("s_win", [128, 8, 4528])
    s_wuq = dscr("s_wuq", [128, 2, 768])
    s_wuqr = dscr("s_wuqr", [128, 2, 768])
    s_wukv = dscr("s_wukv", [128, 1, 1024])
    s_woa = dscr("s_woa", [128, 4, D])
    s_wom = dscr("s_wom", [128, 4, D])
    s_wout = dscr("s_wout", [128, 8, D])
    s_wup = dscr("s_wup", [NFT, 128, 8, 128])
    s_wgate = dscr("s_wgate", [NFT, 128, 8, 128])
    s_wdown = dscr("s_wdown", [128, NFT, D])
    s_h2 = dscr("s_h2", [L, D], F32)
    bh2 = [Buf() for _ in range(NT)]
    bscr = Buf()

    import os
    KSUB = int(os.environ.get("KSUB", "0"))
    QB = int(os.environ.get("QB", "2"))
    S = Sched(nc)
    A = Arena(nc)
    P2 = [nc.alloc_psum_tensor("ps%d" % i, [128, 1024], F32) for i in range(3)]
    PS = []
    for k_ in range(3):
        PS.append(T(P2[k_][:, 0:512]))
        PS.append(T(P2[k_][:, 512:1024]))
    PB = [T(nc.alloc_psum_tensor("pb%d" % i, [128, 1024], BF16)) for i in range(2)]

    for t_ in PS + PB:
        t_.b.excl = True

    def psb(i, c0=0, n=1024):
        return PB[i].h[:, c0:c0 + n]

    def mm(out, lhsT, rhs, start, stop, r, w, skip=False):
        S.op("pe", lambda e, o=out, l=lhsT, rh=rhs, s=start, t=stop, k=skip:
             e.matmul(o, l, rh, start=s, stop=t, skip_group_check=k), r=r, w=w)

    def tr(out, in_, idn, r, w):
        S.op("pe", lambda e, o=out, i=in_, d=idn: e.transpose(o, i, d), r=r, w=w)

    def act(out, in_, func, r, w, bias=None, scale=None, accum=None, eng="act"):
        kw = {}
        if bias is not None:
            kw["bias"] = bias
        if scale is not None:
            kw["scale"] = scale
        if accum is not None:
            kw["accum_out"] = accum
        S.op("act", lambda e, o=out, i=in_, f=func, k=kw: e.activation(o, i, f, **k), r=r, w=w)

    def ts(out, in_, s1, s2, op0, op1, r, w, eng="dve"):
        if op1 is None:
            S.op(eng, lambda e, o=out, i=in_, a=s1, p=op0: e.tensor_scalar(o, i, a, None, p), r=r, w=w)
        else:
            S.op(eng, lambda e, o=out, i=in_, a=s1, b=s2, p=op0, q=op1: e.tensor_scalar(o, i, a, b, p, q), r=r, w=w)

    def tt(out, a, b, op, r, w, eng="dve"):
        S.op(eng, lambda e, o=out, x=a, z=b, p=op: e.tensor_tensor(o, x, z, p), r=r, w=w)

    def stt(out, in0, sc, in1, op0, op1, r, w):
        S.op("dve", lambda e, o=out, a=in0, s=sc, b=in1, p=op0, q=op1:
             e.scalar_tensor_tensor(o, a, s, b, p, q), r=r, w=w)

    def cp(out, in_, r, w, eng="dve"):
        if eng == "act":
            S.op("act", lambda e, o=out, i=in_: e.activation(o, i, AF.Copy), r=r, w=w)
        else:
            S.op(eng, lambda e, o=out, i=in_: e.tensor_copy(o, i), r=r, w=w)

    def memset(t_ap, val, w, eng="dve"):
        S.op(eng, lambda e, o=t_ap, v=val: e.memset(o, v), r=(), w=w)

    def recip(out, in_, r, w):
        S.op("dve", lambda e, o=out, i=in_: e.reciprocal(o, i), r=r, w=w)

    def rstd_from_ss(rs, ss, dim, r, w):
        act(rs, ss, AF.Ln, r=r, w=w, bias=EPS, scale=1.0 / dim)
        act(rs, rs, AF.Exp, r=w, w=w, scale=-0.5)

    ident = A.alloc([128, 128], BF16)
    cm = A.alloc([128, 5, 128], F32)
    ones_f = A.alloc([128, 128], F32)
    gfin = A.alloc([128, D], F32)
    bg = A.alloc([128, 16], F32)
    cvp = A.alloc([128, NFT, 4], F32)
    gm = A.alloc([128, 8], F32)
    gq = A.alloc([128, 2], F32)
    gkv = A.alloc([128, 1], F32)
    gml = A.alloc([128, 4], F32)
    gff = A.alloc([128, 8], F32)
    identf = A.alloc([128, 128], F32)
    Cf0 = A.alloc([128, 4, 129], F32)
    Cb0 = A.alloc([128, 4, 129], F32)
    S.dma("sp", cm[:], cmat, w=[cm.b])
    S.dma("sp", gfin[:], g_fin, w=[gfin.b])
    S.dma("sp", bg[:], bgate, w=[bg.b])
    S.dma("sp", cvp[:], convp, w=[cvp.b])
    S.dma("sp", gm[:], g_mix, w=[gm.b])
    S.dma("sp", gq[:], g_q, w=[gq.b])
    S.dma("sp", gkv[:], g_kv, w=[gkv.b])
    S.dma("sp", gml[:], g_ml, w=[gml.b])
    S.dma("sp", gff[:], g_ffn, w=[gff.b])
    memset(ones_f[:], 1.0, w=[ones_f.b])
    identsrc = din("identsrc", [128, 128])
    S.dma("sp", identf[:], identsrc, w=[identf.b])
    cp(ident[:], identf[:], r=[identf.b], w=[ident.b])
    cmb = A.alloc([128, 5, 128], BF16)
    ones_b = A.alloc([128, 128], BF16)
    cp(cmb[:], cm[:], r=[cm.b], w=[cmb.b])
    memset(ones_b[:], 1.0, w=[ones_b.b])
    UB_f64 = cmb.h[0:64, 0, 0:64]
    UB_b64 = cmb.h[0:64, 1, 0:64]
    UB128 = cmb.h[:, 4, :]
    U_f64 = cm.h[0:64, 0, 0:64]
    U_b64 = cm.h[0:64, 1, 0:64]
    MT_f = cm.h[0:64, 2, 0:64]
    MT_b = cm.h[0:64, 3, 0:64]
    U128 = cm.h[:, 4, :]

    m0 = A.mark()
    stg = [A.alloc([128, 1024], F32) for _ in range(3)]
    stb = [A.alloc([128, 1024], BF16) for _ in range(3)]
    cnt = [0]

    def prep(src, K, N, gain, dst_fn, post=None):
        for kt in range(K // 128):
            for c0, cb in blocks_of(N, 1024):
                i = cnt[0] % 3
                cnt[0] += 1
                a, bq = stg[i], stb[i]
                S.dma("sp", a.h[:, 0:cb], src[kt * 128:(kt + 1) * 128, c0:c0 + cb], w=[a.b])
                if gain is not None:
                    if cnt[0] % 2 == 0:
                        ts(bq.h[:, 0:cb], a.h[:, 0:cb], gain.h[:, kt:kt + 1], None, ALU.mult, None,
                           r=[a.b, gain.b], w=[bq.b])
                    else:
                        act(bq.h[:, 0:cb], a.h[:, 0:cb], AF.Copy, r=[a.b, gain.b], w=[bq.b], scale=gain.h[:, kt:kt + 1])
                else:
                    cp(bq.h[:, 0:cb], a.h[:, 0:cb], r=[a.b], w=[bq.b], eng=("dve" if cnt[0] % 2 == 0 else "act"))
                for (dst, lo, n) in dst_fn(kt, c0, cb):
                    S.dma("act", dst, bq.h[:, lo:lo + n], r=[bq.b], w=[bscr])

    prep(w_in, D, 4528, gm, lambda kt, c0, cb: [(s_win[:, kt, c0:c0 + cb], 0, cb)])
    prep(w_ukv, 128, 1024, gkv, lambda kt, c0, cb: [(s_wukv[:, kt, c0:c0 + cb], 0, cb)])
    prep(w_oa, 512, D, None, lambda kt, c0, cb: [(s_woa[:, kt, c0:c0 + cb], 0, cb)])
    prep(w_om, 512, D, gml, lambda kt, c0, cb: [(s_wom[:, kt, c0:c0 + cb], 0, cb)])
    prep(w_out, D, D, None, lambda kt, c0, cb: [(s_wout[:, kt, c0:c0 + cb], 0, cb)])

    def ffn_dst(sc):
        def f(kt, c0, cb):
            return [(sc[c0 // 128:(c0 + cb) // 128, :, kt, :].rearrange("f p c -> p f c"), 0, cb)]
        return f

    def prep_ffn(src, sc):
        for kt in range(8):
            for c0, cb in blocks_of(DFF, 1024):
                i = cnt[0] % 3
                cnt[0] += 1
                a, bq = stg[i], stb[i]
                S.dma("sp", a.h[:, 0:cb], src[kt * 128:(kt + 1) * 128, c0:c0 + cb], w=[a.b])
                if cnt[0] % 2 == 0:
                    ts(bq.h[:, 0:cb], a.h[:, 0:cb], gff.h[:, kt:kt + 1], None, ALU.mult, None,
                       r=[a.b, gff.b], w=[bq.b])
                else:
                    act(bq.h[:, 0:cb], a.h[:, 0:cb], AF.Copy, r=[a.b, gff.b], w=[bq.b], scale=gff.h[:, kt:kt + 1])
                S.dma("act", sc[c0 // 128:(c0 + cb) // 128, :, kt, :].rearrange("f p c -> p f c"),
                      bq.h[:, 0:cb].rearrange("p (f c) -> p f c", c=128), r=[bq.b], w=[bscr])

    prep_ffn(w_up, s_wup)
    prep_ffn(w_gate, s_wgate)
    prep(w_down, DFF, D, None, lambda kt, c0, cb: [(s_wdown[:, kt, c0:c0 + cb], 0, cb)])
    for kt in range(2):
        i = cnt[0] % 3
        cnt[0] += 1
        a, bq = stg[i], stb[i]
        S.dma("sp", a.h[:, 0:768], w_uq[kt * 128:(kt + 1) * 128, :], w=[a.b])
        ts(bq.h[:, 0:768], a.h[:, 0:768], gq.h[:, kt:kt + 1], None, ALU.mult, None, r=[a.b, gq.b], w=[bq.b])
        S.dma("act", s_wuq[:, kt, :], bq.h[:, 0:768], r=[bq.b], w=[bscr])
        i2 = cnt[0] % 3
        cnt[0] += 1
        b2 = stb[i2]
        a3 = a.h[:, 0:768].rearrange("p (h c) -> p h c", c=96)
        o3 = b2.h[:, 0:768].rearrange("p (h c) -> p h c", c=96)
        ts(b2.h[:, 0:768], a.h[:, 0:768], gq.h[:, kt:kt + 1], None, ALU.mult, None, r=[a.b, gq.b], w=[b2.b])
        ts(o3[:, :, 64:80], a3[:, :, 80:96], gq.h[:, kt:kt + 1], -1.0, ALU.mult, ALU.mult, r=[a.b, gq.b], w=[b2.b])
        ts(o3[:, :, 80:96], a3[:, :, 64:80], gq.h[:, kt:kt + 1], None, ALU.mult, None, r=[a.b, gq.b], w=[b2.b])
        S.dma("act", s_wuqr[:, kt, :], b2.h[:, 0:768], r=[b2.b], w=[bscr])
    S.barrier()
    A.release(m0)

    def front_tile(src_rows, xt, ubf, ssb, rsb, junk, psbank, ut_out, ut_bufs, validcol=None):
        S.dma("sp", xt[:], src_rows, w=[xt.b])
        act(junk[:], xt[:], AF.Square, r=[xt.b], w=[junk.b, ssb.b], accum=ssb.h[:, 0:1])
        act(rsb.h[:, 0:1], ssb.h[:, 0:1], AF.Ln, r=[ssb.b], w=[rsb.b], bias=EPS, scale=1.0 / D)
        act(rsb.h[:, 0:1], rsb.h[:, 0:1], AF.Exp, r=[rsb.b], w=[rsb.b], scale=-0.5)
        ts(ubf[:], xt[:], rsb.h[:, 0:1], None, ALU.mult, None, r=[xt.b, rsb.b], w=[ubf.b])
        if KSUB == 1:
            return
        for kt in range(8):
            tr(psb(0, kt * 128, 128), ubf.h[:, kt * 128:(kt + 1) * 128], ident[:],
               r=[ubf.b, ident.b], w=[PB[0].b])
        cp(ut_out, psb(0, 0, 1024).rearrange("p (k t) -> p k t", k=8), r=[PB[0].b], w=ut_bufs)

    uT = A.alloc_top([128, 8, L], BF16, nb=NT)
    m_slice = A.mark()
    qnT = hmT = attnT = None

    def scan_pass():
        m = A.mark()
        xb = [A.alloc([128, D], F32) for _ in range(2)]
        ubf = [A.alloc([128, D], BF16) for _ in range(2)]
        junk = A.alloc([128, D], BF16)
        ssb = [A.alloc([128, 2], F32) for _ in range(2)]
        rsb = [A.alloc([128, 2], F32) for _ in range(2)]
        ut = [A.alloc([128, 8, 128], BF16) for _ in range(2)]
        wpre = A.alloc([128, 8, 1040], BF16)
        a8 = A.alloc([128, NKF, 8], F32)
        vfull = A.alloc([128, NKF], F32)
        vaug = [A.alloc([128, 4, 130], BF16) for _ in range(2)]
        kw = [A.alloc([128, 128], BF16) for _ in range(4)]
        zg = A.alloc([128, 16], F32)
        lf = A.alloc([128, 8], F32)
        e8 = A.alloc([128, 8], F32)
        al8 = A.alloc([128, 8], F32)
        arg = A.alloc([128, 8], F32)
        gam = A.alloc([128, 4], F32)
        cs16 = A.alloc([128, 16], F32)
        lfh = A.alloc([128, 16], BF16)
        lfr = A.alloc([128, 8], F32)
        arg2 = A.alloc([128, 8], F32)
        S.dma("sp", wpre.h[:, :, 0:16], s_win[:, :, 2464:2480], r=[bscr], w=[wpre.b])
        S.dma("sp", wpre.h[:, :, 16:528], s_win[:, :, 928:1440], r=[bscr], w=[wpre.b])
        S.dma("sp", wpre.h[:, :, 528:1040], s_win[:, :, 1440:1952], r=[bscr], w=[wpre.b])
        S.dma("sp", a8[:], act8, w=[a8.b])
        S.dma("sp", vfull[:], validfull, w=[vfull.b])
        memset(Cf0[:], 0.0, w=[Cf0.b])
        memset(Cb0[:], 0.0, w=[Cb0.b])
        memset(gam[:], 1.0, w=[gam.b])
        for v in vaug:
            memset(v[:], 1.0, w=[v.b])
        ksb = [A.alloc([128, 512], BF16) for _ in range(3)]
        va3 = [A.alloc([128, 4, 130], BF16) for _ in range(3)]
        for v in va3:
            memset(v[:], 1.0, w=[v.b])
        e8s = [A.alloc([128, 8], F32) for _ in range(2)]
        al8s = [A.alloc([128, 8], F32) for _ in range(2)]
        zgs = [A.alloc([128, 16], F32) for _ in range(2)]
        lfs = [A.alloc([128, 8], F32) for _ in range(2)]
        lfhs = [A.alloc([128, 16], BF16) for _ in range(2)]

        def stageA(j):
            p = j % 2
            front_tile(xfull[j * 128:(j + 1) * 128, :], xb[p], ubf[p], ssb[p], rsb[p], junk, 0,
                       ut[p][:], [ut[p].b])

        def stageBmm(j):
            p = j % 2
            u = ut[p]
            pg_ = PS[p]
            for kt in range(8):
                mm(pg_.h[:, 0:16], u.h[:, kt, :], wpre.h[:, kt, 0:16], kt == 0, kt == 7, r=[u.b, wpre.b], w=[pg_.b])
            for kt in range(8):
                mm(PS[2].h[:, :], u.h[:, kt, :], wpre.h[:, kt, 16:528], kt == 0, kt == 7, r=[u.b, wpre.b], w=[PS[2].b])
            for kt in range(8):
                mm(PS[3].h[:, :], u.h[:, kt, :], wpre.h[:, kt, 528:1040], kt == 0, kt == 7, r=[u.b, wpre.b], w=[PS[3].b])

        def stageBrest(j):
            p = j % 2
            q3 = j % 3
            pg_ = PS[p]
            zg_, lf_, lfh_ = zgs[p], lfs[p], lfhs[p]
            act(ksb[q3][:], PS[2].h[:, :], AF.Copy, r=[PS[2].b], w=[ksb[q3].b], scale=SCALE_K)
            cp(va3[q3].h[:, :, 0:128], PS[3].h[:, :].rearrange("p (h c) -> p h c", h=4), r=[PS[3].b], w=[va3[q3].b])
            tt(zg_[:], pg_.h[:, 0:16], bg[:], ALU.add, r=[pg_.b, bg.b], w=[zg_.b])
            act(lf_[:], zg_.h[:, 8:16], AF.Exp, r=[zg_.b], w=[lf_.b], scale=-1.0)
            act(lf_[:], lf_[:], AF.Ln, r=[lf_.b], w=[lf_.b], bias=1.0)
            ts(lf_[:], lf_[:], vfull.h[:, j:j + 1], -1.0, ALU.mult, ALU.mult, r=[lf_.b, vfull.b], w=[lf_.b])
            cp(lfh_.h[:, 0:8], lf_[:], r=[lf_.b], w=[lfh_.b])
            tt(lfr[:], lf_[:], lfh_.h[:, 0:8], ALU.subtract, r=[lf_.b, lfh_.b], w=[lfr.b])
            cp(lfh_.h[:, 8:16], lfr[:], r=[lfr.b], w=[lfh_.b])

        def stage2(j):
            p = j % 2
            pg_ = PS[p]
            zg_, lf_, lfh_ = zgs[p], lfs[p], lfhs[p]
            mm(pg_.h[:, 16:24], UB128, lfh_.h[:, 0:8], True, False, r=[cmb.b, lfh_.b], w=[pg_.b])
            mm(pg_.h[:, 16:24], UB128, lfh_.h[:, 8:16], False, True, r=[cmb.b, lfh_.b], w=[pg_.b])
            mm(pg_.h[:, 24:32], ones_b[:], lfh_.h[:, 0:8], True, False, r=[ones_b.b, lfh_.b], w=[pg_.b])
            mm(pg_.h[:, 24:32], ones_b[:], lfh_.h[:, 8:16], False, True, r=[ones_b.b, lfh_.b], w=[pg_.b])
            cp(cs16[:], pg_.h[:, 16:32], r=[pg_.b], w=[cs16.b])
            tt(arg.h[:, 0:4], cs16.h[:, 8:12], cs16.h[:, 0:4], ALU.subtract, r=[cs16.b], w=[arg.b])
            tt(arg.h[:, 4:8], cs16.h[:, 4:8], lf_.h[:, 4:8], ALU.subtract, r=[cs16.b, lf_.b], w=[arg.b])
            tt(arg[:], arg[:], zg_.h[:, 0:8], ALU.add, r=[arg.b, zg_.b], w=[arg.b])
            act(e8s[p][:], arg[:], AF.Exp, r=[arg.b], w=[e8s[p].b])
            tt(e8s[p][:], e8s[p][:], a8.h[:, j, :], ALU.mult, r=[e8s[p].b, a8.b], w=[e8s[p].b])
            tt(arg2[:], cs16.h[:, 8:16], a8.h[:, j, :], ALU.mult, r=[cs16.b, a8.b], w=[arg2.b])
            act(al8s[p][:], arg2[:], AF.Exp, r=[arg2.b], w=[al8s[p].b])

        kw8 = [A.alloc([128, 128], BF16) for _ in range(8)]

        def stage3(j):
            p = j % 2
            q3 = j % 3
            va = va3[q3]
            e8_, al8_ = e8s[p], al8s[p]
            for hd in range(8):
                h = hd % 4
                ts(kw8[hd][:], ksb[q3].h[:, h * 128:(h + 1) * 128], e8_.h[:, hd:hd + 1], None, ALU.mult, None,
                   r=[ksb[q3].b, e8_.b], w=[kw8[hd].b])

        def stage3rest(j):
            p = j % 2
            q3 = j % 3
            va = va3[q3]
            e8_, al8_ = e8s[p], al8s[p]
            for hd in range(8):
                h = hd % 4
                pd_ = PS[2 + (hd % 4)]
                mm(pd_.h[:, 0:129], kw8[hd][:], va.h[:, h, 0:129], True, True, r=[kw8[hd].b, va.b], w=[pd_.b])
                if hd < 4:
                    stt(Cf0.h[:, h, :], Cf0.h[:, h, :], al8_.h[:, h:h + 1], pd_.h[:, 0:129], ALU.mult, ALU.add,
                        r=[Cf0.b, al8_.b, pd_.b], w=[Cf0.b])
                else:
                    stt(Cb0.h[:, h, :], pd_.h[:, 0:129], gam.h[:, h:h + 1], Cb0.h[:, h, :], ALU.mult, ALU.add,
                        r=[Cb0.b, gam.b, pd_.b], w=[Cb0.b])
            tt(gam[:], gam[:], al8_.h[:, 4:8], ALU.mult, r=[gam.b, al8_.b], w=[gam.b])

        for i in range(NKF + 3):
            if 0 <= i - 3 < NKF:
                stage3(i - 3)
            if 0 <= i - 2 < NKF:
                stage2(i - 2)
            if 0 <= i - 1 < NKF:
                stageBmm(i - 1)
            if i < NKF:
                stageA(i)
            if 0 <= i - 1 < NKF:
                stageBrest(i - 1)
            if 0 <= i - 3 < NKF:
                stage3rest(i - 3)
        S.barrier()
        A.release(m)

    def phase0(sl, kvdst=None):
        m = A.mark()
        xb = [A.alloc([128, D], F32) for _ in range(2)]
        ubf = [A.alloc([128, D], BF16) for _ in range(2)]
        junk = A.alloc([128, D], BF16)
        ssb = [A.alloc([128, 2], F32) for _ in range(2)]
        rsb = [A.alloc([128, 2], F32) for _ in range(2)]
        NW = 416 if kvdst is not None else 256
        wl = A.alloc([128, 8, NW], BF16)
        qn = [A.alloc([128, 256], BF16) for _ in range(2)]
        S.dma("sp", wl[:], s_win[:, :, 0:NW], r=[bscr], w=[wl.b])
        if kvdst is not None:
            KT_, cnT_ = kvdst
            ssk = [A.alloc([128, 2], F32) for _ in range(2)]
            cst = A.alloc([128, NT, 32], F32)
            cn = [A.alloc([128, 128], BF16) for _ in range(2)]
            kst = [A.alloc([128, 96], BF16) for _ in range(2)]
            tmp = A.alloc([128, 4, 16], F32)
            S.dma("sp", cst[:], cs_tok[sl], w=[cst.b])
            for k in kst:
                memset(k[:], 0.0, w=[k.b])
        def p0A(i):
            p = i % 2
            tsl = slice(i * 128, (i + 1) * 128)
            front_tile(xs[sl, tsl, :], xb[p], ubf[p], ssb[p], rsb[p], junk, 0, uT.h[:, :, tsl], [uT.bs[i]])

        def p0mm(i):
            p = i % 2
            tsl = slice(i * 128, (i + 1) * 128)
            pl = PS[1 + p]
            for kt in range(8):
                mm(pl.h[:, 0:NW], uT.h[:, kt, tsl], wl.h[:, kt, :], kt == 0, kt == 7,
                   r=[uT.bs[i], wl.b], w=[pl.b])

        def p0B(i):
            p = i % 2
            pl = PS[1 + p]
            act(junk.h[:, 0:256], pl.h[:, 0:256], AF.Square, r=[pl.b], w=[junk.b, ssq[p].b],
                accum=ssq[p].h[:, 0:1])
            rstd_from_ss(ssq[p].h[:, 1:2], ssq[p].h[:, 0:1], 256, r=[ssq[p].b], w=[ssq[p].b])
            act(qn[p][:], pl.h[:, 0:256], AF.Copy, r=[pl.b, ssq[p].b], w=[qn[p].b], scale=ssq[p].h[:, 1:2])
            if kvdst is not None:
                act(junk.h[:, 256:384], pl.h[:, 256:384], AF.Square, r=[pl.b], w=[junk.b, ssk[p].b],
                    accum=ssk[p].h[:, 0:1])
                rstd_from_ss(ssk[p].h[:, 1:2], ssk[p].h[:, 0:1], 128, r=[ssk[p].b], w=[ssk[p].b])
                act(cn[p][:], pl.h[:, 256:384], AF.Copy, r=[pl.b, ssk[p].b], w=[cn[p].b], scale=ssk[p].h[:, 1:2])
                x1 = pl.h[:, 384:400]
                x2 = pl.h[:, 400:416]
                co = cst.h[:, i, 0:16]
                si = cst.h[:, i, 16:32]
                tt(tmp.h[:, 0, :], x1, co, ALU.mult, r=[pl.b, cst.b], w=[tmp.b])
                tt(tmp.h[:, 1, :], x2, si, ALU.mult, r=[pl.b, cst.b], w=[tmp.b])
                tt(tmp.h[:, 2, :], x1, si, ALU.mult, r=[pl.b, cst.b], w=[tmp.b])
                tt(tmp.h[:, 3, :], x2, co, ALU.mult, r=[pl.b, cst.b], w=[tmp.b])
                tt(kst[p].h[:, 64:80], tmp.h[:, 0, :], tmp.h[:, 1, :], ALU.subtract, r=[tmp.b], w=[kst[p].b])
                tt(kst[p].h[:, 80:96], tmp.h[:, 2, :], tmp.h[:, 3, :], ALU.add, r=[tmp.b], w=[kst[p].b])

        def p0C(i):
            p = i % 2
            tsl = slice(i * 128, (i + 1) * 128)
            for kt in range(2):
                tr(psb(1, kt * 128, 128), qn[p].h[:, kt * 128:(kt + 1) * 128], ident[:],
                   r=[qn[p].b, ident.b], w=[PB[1].b])
            if kvdst is not None:
                tr(psb(1, 256, 128), cn[p][:], ident[:], r=[cn[p].b, ident.b], w=[PB[1].b])
                tr(PB[1].h[0:96, 384:512], kst[p][:], ident[:], r=[kst[p].b, ident.b], w=[PB[1].b])
            cp(qnT.h[:, :, tsl], psb(1, 0, 256).rearrange("p (k t) -> p k t", k=2), r=[PB[1].b], w=[qnT.bs[i]])
            if kvdst is not None:
                cp(cnT_.h[:, tsl], psb(1, 256, 128), r=[PB[1].b], w=[cnT_.bs[i]])
                cp(KT_.h[64:96, tsl], PB[1].h[64:96, 384:512], r=[PB[1].b], w=[KT_.bs[i]], eng="act")

        ssq = [A.alloc([128, 2], F32) for _ in range(2)]
        for it in range(NT + 2):
            if 0 <= it - 2 < NT:
                p0C(it - 2)
            if 0 <= it - 1 < NT:
                p0mm(it - 1)
            if it < NT:
                p0A(it)
            if 0 <= it - 1 < NT:
                p0B(it - 1)
        S.barrier()
        A.release(m)

    def kv_pass(src_fn, ntiles, cs_src, KT, cnT):
        m = A.mark()
        xb = [A.alloc([128, D], F32) for _ in range(2)]
        ubf = [A.alloc([128, D], BF16) for _ in range(2)]
        junk = A.alloc([128, D], BF16)
        ssb = [A.alloc([128, 2], F32) for _ in range(2)]
        rsb = [A.alloc([128, 2], F32) for _ in range(2)]
        ut = [A.alloc([128, 8, 128], BF16) for _ in range(2)]
        wkv = A.alloc([128, 8, 160], BF16)
        cst = A.alloc([128, ntiles, 32], F32)
        cn = [A.alloc([128, 128], BF16) for _ in range(2)]
        kst = [A.alloc([128, 96], BF16) for _ in range(2)]
        tmp = A.alloc([128, 4, 16], F32)
        S.dma("sp", wkv[:], s_win[:, :, 256:416], r=[bscr], w=[wkv.b])
        S.dma("sp", cst[:], cs_src, w=[cst.b])
        for k in kst:
            memset(k[:], 0.0, w=[k.b])
        def kstage1(j):
            p = j % 2
            front_tile(src_fn(j), xb[p], ubf[p], ssb[p], rsb[p], junk, 0, ut[p][:], [ut[p].b])

        def kstage2mm(j):
            p = j % 2
            u = ut[p]
            pk_ = PS[1 + p]
            for kt in range(8):
                mm(pk_.h[:, 0:160], u.h[:, kt, :], wkv.h[:, kt, :], kt == 0, kt == 7, r=[u.b, wkv.b], w=[pk_.b])

        def kstage2(j):
            p = j % 2
            act(junk.h[:, 0:128], PS[1 + p].h[:, 0:128], AF.Square, r=[PS[1 + p].b], w=[junk.b, ssb[p].b],
                accum=ssb[p].h[:, 1:2])
            rstd_from_ss(rsb[p].h[:, 1:2], ssb[p].h[:, 1:2], 128, r=[ssb[p].b], w=[rsb[p].b])
            act(cn[p][:], PS[1 + p].h[:, 0:128], AF.Copy, r=[PS[1 + p].b, rsb[p].b], w=[cn[p].b], scale=rsb[p].h[:, 1:2])
            x1 = PS[1 + p].h[:, 128:144]
            x2 = PS[1 + p].h[:, 144:160]
            co = cst.h[:, j, 0:16]
            si = cst.h[:, j, 16:32]
            tt(tmp.h[:, 0, :], x1, co, ALU.mult, r=[PS[1 + p].b, cst.b], w=[tmp.b])
            tt(tmp.h[:, 1, :], x2, si, ALU.mult, r=[PS[1 + p].b, cst.b], w=[tmp.b])
            tt(tmp.h[:, 2, :], x1, si, ALU.mult, r=[PS[1 + p].b, cst.b], w=[tmp.b])
            tt(tmp.h[:, 3, :], x2, co, ALU.mult, r=[PS[1 + p].b, cst.b], w=[tmp.b])
            tt(kst[p].h[:, 64:80], tmp.h[:, 0, :], tmp.h[:, 1, :], ALU.subtract, r=[tmp.b], w=[kst[p].b])
            tt(kst[p].h[:, 80:96], tmp.h[:, 2, :], tmp.h[:, 3, :], ALU.add, r=[tmp.b], w=[kst[p].b])

        def kstage3(j):
            p = j % 2
            tr(psb(1, 0, 128), cn[p][:], ident[:], r=[cn[p].b, ident.b], w=[PB[1].b])
            tr(PB[1].h[0:96, 128:256], kst[p][:], ident[:], r=[kst[p].b, ident.b], w=[PB[1].b])
            cp(cnT.h[:, j * 128:(j + 1) * 128], psb(1, 0, 128), r=[PB[1].b], w=[cnT.bs[j]])
            cp(KT.h[64:96, j * 128:(j + 1) * 128], PB[1].h[64:96, 128:256], r=[PB[1].b], w=[KT.bs[j]], eng="act")

        for i in range(ntiles + 2):
            if 0 <= i - 2 < ntiles:
                kstage3(i - 2)
            if 0 <= i - 1 < ntiles:
                kstage2mm(i - 1)
            if i < ntiles:
                kstage1(i)
            if 0 <= i - 1 < ntiles:
                kstage2(i - 1)
        S.barrier()
        A.release(m)

    AXX = mybir.AxisListType.X

    def phase1(sl, sample):
        m = A.mark()
        wq2 = [A.alloc([128, 8, 128], BF16) for _ in range(2)]
        wk2 = [A.alloc([128, 8, 128], BF16) for _ in range(2)]
        wx2 = [A.alloc([128, 8, 388], BF16) for _ in range(2)]

        def load_head_w(hh):
            wq_, wk_, wx_ = wq2[hh % 2], wk2[hh % 2], wx2[hh % 2]
            S.dma("sp", wq_[:], s_win[:, :, 416 + 128 * hh:544 + 128 * hh], r=[bscr], w=[wq_.b])
            S.dma("sp", wk_[:], s_win[:, :, 928 + 128 * hh:1056 + 128 * hh], r=[bscr], w=[wk_.b])
            S.dma("sp", wx_.h[:, :, 0:128], s_win[:, :, 928 + 128 * hh:1056 + 128 * hh], r=[bscr], w=[wx_.b])
            S.dma("sp", wx_.h[:, :, 128:256], s_win[:, :, 1440 + 128 * hh:1568 + 128 * hh], r=[bscr], w=[wx_.b])
            S.dma("sp", wx_.h[:, :, 256:384], s_win[:, :, 1952 + 128 * hh:2080 + 128 * hh], r=[bscr], w=[wx_.b])
        wg16 = A.alloc([128, 8, 16], BF16)
        qT = A.alloc([128, L], BF16)
        kT = A.alloc([128, L], BF16)
        kh = A.alloc([64, NCH, 128], BF16, nb=NCH)
        va = A.alloc([64, NCH, 130], BF16, nb=NCH)
        so = A.alloc([64, NCH, 128], BF16, nb=NCH)
        Cbf = A.alloc([128, NCH + 1, 130], BF16, nb=NCH + 1)
        Cbb = A.alloc([128, NCH + 1, 130], BF16, nb=NCH + 1)
        Cf = A.alloc([128, 129], F32)
        Cb = A.alloc([128, 129], F32)
        zall = A.alloc([64, NCH, 4], F32, nb=NCH)
        lf3 = A.alloc([64, NCH, 2], F32)
        lh3 = A.alloc([64, 2, NCH, 2], BF16)
        res3 = A.alloc([64, NCH, 2], F32)
        bb3 = A.alloc([64, NCH, 2], F32)
        d3 = A.alloc([64, NCH, 2], F32)
        ew3 = A.alloc([64, NCH, 2], F32)
        eb3 = A.alloc([64, NCH, 2], F32)
        ew23 = A.alloc([64, NCH, 2], F32)
        eg3 = A.alloc([128, NCH, 2], F32)
        v64 = A.alloc([64, NCH], F32)
        kw = [A.alloc([64, 128], BF16) for _ in range(6)]
        sp_ = [A.alloc([64, 2, 64], BF16) for _ in range(3)]
        stage = A.alloc([64, NCH, 2, 130], F32, nb=NCH)
        den3 = A.alloc([64, NCH, 2], F32)
        neg3 = A.alloc([64, NCH, 2], F32)
        ss3 = A.alloc([64, NCH], F32)
        bgh = A.alloc([64, 4], F32)
        S.dma("sp", v64[:], valid64[sl], w=[v64.b])
        S.dma("sp", wg16[:], s_win[:, :, 2464:2480], r=[bscr], w=[wg16.b])
        NC2 = 2 * NCH
        for h in range(4):
            wq, wk, wx = wq2[h % 2], wk2[h % 2], wx2[h % 2]
            if h == 0:
                load_head_w(0)
            cp(wx.h[:, :, 384:388], wg16.h[:, :, h:16:4], r=[wg16.b], w=[wx.b], eng="pool")
            cp(bgh[:], bg.h[0:64, h:16:4], r=[bg.b], w=[bgh.b], eng="pool")
            memset(va.h[:, :, 128:130], 1.0, w=va.bs)
            for (t0, tn) in blocks_of(L, 512):
                tl = list(range(t0 // 128, (t0 + tn) // 128))
                for kt in range(8):
                    mm(PS[0].h[:, 0:tn], wq.h[:, kt, :], uT.h[:, kt, t0:t0 + tn], kt == 0, kt == 7,
                       r=[wq.b] + [uT.bs[i] for i in tl], w=[PS[0].b])
                cp(qT.h[:, t0:t0 + tn], PS[0].h[:, 0:tn], r=[PS[0].b], w=[qT.b], eng="act")
                for kt in range(8):
                    mm(PS[1].h[:, 0:tn], wk.h[:, kt, :], uT.h[:, kt, t0:t0 + tn], kt == 0, kt == 7,
                       r=[wk.b] + [uT.bs[i] for i in tl], w=[PS[1].b])
                ts(kT.h[:, t0:t0 + tn], PS[1].h[:, 0:tn], SCALE_K, None, ALU.mult, None, r=[PS[1].b], w=[kT.b])
            if sample:
                cp(Cf[:], Cf0.h[:, h, :], r=[Cf0.b], w=[Cf.b])
                cp(Cb[:], Cb0.h[:, h, :], r=[Cb0.b], w=[Cb.b])
            else:
                memset(Cf[:], 0.0, w=[Cf.b])
                memset(Cb[:], 0.0, w=[Cb.b])
            cp(Cbf.h[:, 0, 0:129], Cf[:], r=[Cf.b], w=[Cbf.bs[0]], eng="act")
            cp(Cbb.h[:, NCH, 0:129], Cb[:], r=[Cb.b], w=[Cbb.bs[NCH]], eng="act")
            for n in range(NCH):
                cs_ = slice(n * 64, (n + 1) * 64)
                pa = PS[2 + (n % 4)]
                for kt in range(8):
                    mm(pa.h[0:64, 0:388], uT.h[:, kt, cs_], wx.h[:, kt, :], kt == 0, kt == 7,
                       r=[uT.bs[n // 2], wx.b], w=[pa.b])
                act(so.h[:, n, :], pa.h[0:64, 256:384], AF.Sigmoid, r=[pa.b], w=[so.bs[n]])
                ts(kh.h[:, n, :], pa.h[0:64, 0:128], SCALE_K, None, ALU.mult, None, r=[pa.b], w=[kh.bs[n]])
                cp(va.h[:, n, 0:128], pa.h[0:64, 128:256], r=[pa.b], w=[va.bs[n]])
                tt(zall.h[:, n, :], pa.h[0:64, 384:388], bgh[:], ALU.add, r=[pa.b, bgh.b], w=[zall.bs[n]])
            if h + 1 < 4:
                load_head_w(h + 1)
            zb = zall.bs
            act(lf3[:], zall.h[:, :, 2:4], AF.Exp, r=zb, w=[lf3.b], scale=-1.0)
            act(lf3[:], lf3[:], AF.Ln, r=[lf3.b], w=[lf3.b], bias=1.0)
            ts(lf3[:], lf3[:], -1.0, None, ALU.mult, None, r=[lf3.b], w=[lf3.b])
            tt(lf3[:], lf3[:], v64.h[:, :].unsqueeze(2).broadcast_to([64, NCH, 2]), ALU.mult, r=[lf3.b, v64.b], w=[lf3.b])
            cp(lh3.h[:, 0, :, :], lf3[:], r=[lf3.b], w=[lh3.b])
            tt(res3[:], lf3[:], lh3.h[:, 0, :, :], ALU.subtract, r=[lf3.b, lh3.b], w=[res3.b])
            cp(lh3.h[:, 1, :, :], res3[:], r=[res3.b], w=[lh3.b])
            hi2 = lh3.h[:, 0, :, :].rearrange("p n c -> p (n c)")
            lo2 = lh3.h[:, 1, :, :].rearrange("p n c -> p (n c)")
            pc = PS[0]
            mm(pc.h[0:64, 0:NC2], UB_f64, hi2, True, False, r=[cmb.b, lh3.b], w=[pc.b])
            mm(pc.h[0:64, 0:NC2], UB_f64, lo2, False, True, r=[cmb.b, lh3.b], w=[pc.b])
            mm(pc.h[0:64, NC2:2 * NC2], UB_b64, hi2, True, False, r=[cmb.b, lh3.b], w=[pc.b])
            mm(pc.h[0:64, NC2:2 * NC2], UB_b64, lo2, False, True, r=[cmb.b, lh3.b], w=[pc.b])
            mm(pc.h[:, 2 * NC2:3 * NC2], ones_b.h[0:64, :], hi2, True, False, r=[ones_b.b, lh3.b], w=[pc.b])
            mm(pc.h[:, 2 * NC2:3 * NC2], ones_b.h[0:64, :], lo2, False, True, r=[ones_b.b, lh3.b], w=[pc.b])
            pcF = pc.h[0:64, 0:NC2].rearrange("p (n c) -> p n c", c=2)
            pcB = pc.h[0:64, NC2:2 * NC2].rearrange("p (n c) -> p n c", c=2)
            pcT = pc.h[:, 2 * NC2:3 * NC2].rearrange("p (n c) -> p n c", c=2)
            cp(bb3.h[:, :, 0:1], pcF[:, :, 0:1], r=[pc.b], w=[bb3.b])
            cp(bb3.h[:, :, 1:2], pcB[:, :, 1:2], r=[pc.b], w=[bb3.b])
            act(eg3[:], pcT, AF.Exp, r=[pc.b], w=[eg3.b])
            tt(d3[:], zall.h[:, :, 0:2], bb3[:], ALU.subtract, r=zb + [bb3.b], w=[d3.b])
            act(ew3[:], d3[:], AF.Exp, r=[d3.b], w=[ew3.b])
            act(eb3[:], bb3[:], AF.Exp, r=[bb3.b], w=[eb3.b])
            tt(ew23[:], ew3[:], eg3.h[0:64, :, :], ALU.mult, r=[ew3.b, eg3.b], w=[ew23.b])
            for i in range(NCH + 1):
                if i < NCH:
                    nf, nb_ = i, NCH - 1 - i
                    kf = kw[(2 * i) % 6]
                    kb = kw[(2 * i + 1) % 6]
                    ts(kf[:], kh.h[:, nf, :], ew23.h[:, nf, 0:1], None, ALU.mult, None, r=[kh.bs[nf], ew23.b], w=[kf.b])
                    ts(kb[:], kh.h[:, nb_, :], ew23.h[:, nb_, 1:2], None, ALU.mult, None, r=[kh.bs[nb_], ew23.b], w=[kb.b])
                    pdf = PS[1 + (i % 2)]
                    pdb = PS[3 + (i % 2)]
                    mm(pdf.h[:, 0:129], kf[:], va.h[:, nf, 0:129], True, True, r=[kf.b, va.bs[nf]], w=[pdf.b])
                    mm(pdb.h[:, 0:129], kb[:], va.h[:, nb_, 0:129], True, True, r=[kb.b, va.bs[nb_]], w=[pdb.b])
                j = i - 1
                if j >= 0:
                    nf, nb_ = j, NCH - 1 - j
                    pdf = PS[1 + (j % 2)]
                    pdb = PS[3 + (j % 2)]
                    stt(Cf[:], Cf[:], eg3.h[:, nf, 0:1], pdf.h[:, 0:129], ALU.mult, ALU.add, r=[Cf.b, eg3.b, pdf.b], w=[Cf.b])
                    stt(Cb[:], Cb[:], eg3.h[:, nb_, 1:2], pdb.h[:, 0:129], ALU.mult, ALU.add, r=[Cb.b, eg3.b, pdb.b], w=[Cb.b])
                    cp(Cbf.h[:, nf + 1, 0:129], Cf[:], r=[Cf.b], w=[Cbf.bs[nf + 1]], eng="act")
                    cp(Cbb.h[:, nb_, 0:129], Cb[:], r=[Cb.b], w=[Cbb.bs[nb_]], eng="act")
            for i in range(NCH + 2):
                if i < NCH:
                    cs_ = slice(i * 64, (i + 1) * 64)
                    pq = PS[i % 2]
                    mm(pq.h[0:64, 0:64], kT.h[:, cs_], qT.h[:, cs_], True, True, r=[kT.b, qT.b], w=[pq.b])
                n = i - 1
                if 0 <= n < NCH:
                    cs_ = slice(n * 64, (n + 1) * 64)
                    pq = PS[n % 2]
                    s_ = sp_[n % 3]
                    stt(s_.h[:, 0, :], pq.h[0:64, 0:64], ew3.h[:, n, 0:1], MT_f, ALU.mult, ALU.mult,
                        r=[pq.b, ew3.b, cm.b], w=[s_.b])
                    stt(s_.h[:, 1, :], pq.h[0:64, 0:64], ew3.h[:, n, 1:2], MT_b, ALU.mult, ALU.mult,
                        r=[pq.b, ew3.b, cm.b], w=[s_.b])
                    po = PS[2 + (n % 2)]
                    mm(po.h[0:64, 0:129], s_.h[:, 0, :], va.h[:, n, 0:129], True, False, r=[s_.b, va.bs[n]], w=[po.b])
                    mm(po.h[0:64, 0:129], qT.h[:, cs_], Cbf.h[:, n, 0:129], False, True, r=[qT.b, Cbf.bs[n]], w=[po.b])
                    po2 = PS[4 + (n % 2)]
                    mm(po2.h[0:64, 0:129], s_.h[:, 1, :], va.h[:, n, 0:129], True, False, r=[s_.b, va.bs[n]], w=[po2.b])
                    mm(po2.h[0:64, 0:129], qT.h[:, cs_], Cbb.h[:, n + 1, 0:129], False, True,
                       r=[qT.b, Cbb.bs[n + 1]], w=[po2.b])
                n = i - 2
                if 0 <= n < NCH:
                    po = PS[2 + (n % 2)]
                    po2 = PS[4 + (n % 2)]
                    cp(stage.h[:, n, 0, 0:129], po.h[0:64, 0:129], r=[po.b], w=[stage.bs[n]], eng="act")
                    cp(stage.h[:, n, 1, 0:129], po2.h[0:64, 0:129], r=[po2.b], w=[stage.bs[n]])
            sb_all = stage.bs
            num0 = stage.h[:, :, 0, 0:128]
            num1 = stage.h[:, :, 1, 0:128]
            tt(den3[:], stage.h[:, :, :, 128], eb3[:], ALU.mult, r=sb_all + [eb3.b], w=[den3.b])
            ts(neg3[:], den3[:], -1.0, None, ALU.mult, None, r=[den3.b], w=[neg3.b])
            tt(den3[:], den3[:], neg3[:], ALU.max, r=[den3.b, neg3.b], w=[den3.b])
            ts(den3[:], den3[:], 1.0, None, ALU.max, None, r=[den3.b], w=[den3.b])
            recip(den3[:], den3[:], r=[den3.b], w=[den3.b])
            tt(den3[:], den3[:], eb3[:], ALU.mult, r=[den3.b, eb3.b], w=[den3.b])
            tt(num0, num0, den3.h[:, :, 0:1].broadcast_to([64, NCH, 128]), ALU.mult, r=sb_all + [den3.b], w=sb_all)
            tt(num1, num1, den3.h[:, :, 1:2].broadcast_to([64, NCH, 128]), ALU.mult, r=sb_all + [den3.b], w=sb_all)
            tt(num0, num0, num1, ALU.add, r=sb_all, w=sb_all)
            tt(num1, num0, num0, ALU.mult, r=sb_all, w=sb_all)
            S.op("dve", lambda e, o=ss3[:], i=num1: e.tensor_reduce(o, i, AXX, ALU.add), r=sb_all, w=[ss3.b])
            rstd_from_ss(ss3[:], ss3[:], 128, r=[ss3.b], w=[ss3.b])
            tt(num0, num0, ss3.h[:, :].unsqueeze(2).broadcast_to([64, NCH, 128]), ALU.mult, r=sb_all + [ss3.b], w=sb_all)
            tt(kh[:], num0, so[:], ALU.mult, r=sb_all + so.bs, w=kh.bs)
            for (n0, nn) in blocks_of(NCH, 8):
                pb_ = PB[(n0 // 8) % 2]
                for n in range(n0, n0 + nn):
                    tr(pb_.h[:, (n - n0) * 64:(n - n0 + 1) * 64], kh.h[:, n, :], ident.h[0:64, 0:64],
                       r=[kh.bs[n], ident.b], w=[pb_.b])
                cp(hmT.h[:, h, n0 * 64:(n0 + nn) * 64], pb_.h[:, 0:nn * 64], r=[pb_.b], w=[hmT.b], eng="act")
        S.barrier()
        A.release(m)

    def phase2(sl, nkt, KT, cnT, kb_src, prefetch=None):
        m = A.mark()
        wuq = A.alloc([128, 2, 768], BF16)
        wuqr = A.alloc([128, 2, 768], BF16)
        wukv = A.alloc([128, 1, 1024], BF16)
        cT = A.alloc([96, 2, L], F32)
        kbs = A.alloc([128, nkt], F32)
        Vh = A.alloc([128, nkt, 66], BF16)
        QT = [A.alloc([96, L], BF16) for _ in range(2)]
        attok = A.alloc([128, NT, 128], BF16, nb=NT)
        PT2 = [A.alloc([128, 2, 512], BF16) for _ in range(3)]
        t1 = A.alloc([96, 512], F32)
        t2 = A.alloc([96, 512], F32)
        rd = [A.alloc([128, 4], F32) for _ in range(2)]
        S.dma("sp", wuq[:], s_wuq, r=[bscr], w=[wuq.b])
        S.dma("sp", wuqr[:], s_wuqr, r=[bscr], w=[wuqr.b])
        S.dma("sp", wukv[:], s_wukv, r=[bscr], w=[wukv.b])
        S.dma("sp", cT.h[64:96, 0, :], csT[sl, 0], w=[cT.b])
        S.dma("sp", cT.h[64:96, 1, :], csT[sl, 1], w=[cT.b])
        S.dma("sp", kbs[:], kb_src, w=[kbs.b])
        if prefetch is not None:
            prefetch()
        memset(Vh[:], 1.0, w=[Vh.b])
        qblocks = blocks_of(L, 512)
        sctr = 0
        for h in range(8):
            for bix, (k0, kn) in enumerate(blocks_of(nkt * 128, 512)):
                tl = list(range(k0 // 128, (k0 + kn) // 128))
                pk_ = PS[bix % 4]
                mm(pk_.h[0:64, 0:kn], wukv.h[:, 0, h * 128:h * 128 + 64], cnT.h[:, k0:k0 + kn], True, True,
                   r=[wukv.b] + [cnT.bs[i] for i in tl], w=[pk_.b])
                cp(KT.h[0:64, k0:k0 + kn], pk_.h[0:64, 0:kn], r=[pk_.b], w=[KT.bs[i] for i in tl],
                   eng=("act" if bix % 2 else "dve"))
            for bix, (j0, jn) in enumerate(blocks_of(nkt, 8)):
                pv_ = PS[bix % 4]
                for j in range(j0, j0 + jn):
                    mm(pv_.h[:, (j - j0) * 64:(j - j0 + 1) * 64], cnT.h[:, j * 128:(j + 1) * 128],
                       wukv.h[:, 0, h * 128 + 64:h * 128 + 128], True, True, r=[cnT.bs[j], wukv.b], w=[pv_.b])
                cp(Vh.h[:, j0:j0 + jn, 0:64], pv_.h[:, 0:jn * 64].rearrange("p (j c) -> p j c", c=64),
                   r=[pv_.b], w=[Vh.b], eng=("dve" if bix % 2 else "act"))
            Q = QT[h % 2]
            for (q0, qn_) in qblocks:
                tl = list(range(q0 // 128, (q0 + qn_) // 128))
                for kt in range(2):
                    mm(PS[2].h[0:96, 0:qn_], wuq.h[:, kt, h * 96:(h + 1) * 96], qnT.h[:, kt, q0:q0 + qn_],
                       kt == 0, kt == 1, r=[wuq.b] + [qnT.bs[i] for i in tl], w=[PS[2].b])
                for kt in range(2):
                    mm(PS[3].h[0:96, 0:qn_], wuqr.h[:, kt, h * 96:(h + 1) * 96], qnT.h[:, kt, q0:q0 + qn_],
                       kt == 0, kt == 1, r=[wuqr.b] + [qnT.bs[i] for i in tl], w=[PS[3].b])
                cp(Q.h[0:64, q0:q0 + qn_], PS[2].h[0:64, 0:qn_], r=[PS[2].b], w=[Q.b], eng="act")
                tt(t1.h[64:96, 0:qn_], PS[2].h[64:96, 0:qn_], cT.h[64:96, 0, q0:q0 + qn_], ALU.mult,
                   r=[PS[2].b, cT.b], w=[t1.b])
                tt(t2.h[64:96, 0:qn_], PS[3].h[64:96, 0:qn_], cT.h[64:96, 1, q0:q0 + qn_], ALU.mult,
                   r=[PS[3].b, cT.b], w=[t2.b])
                tt(Q.h[64:96, q0:q0 + qn_], t1.h[64:96, 0:qn_], t2.h[64:96, 0:qn_], ALU.add,
                   r=[t1.b, t2.b], w=[Q.b])
            groups = [[0]]
            k_ = 1
            while k_ < nkt - 1:
                if k_ + 1 < nkt - 1:
                    groups.append([k_, k_ + 1])
                    k_ += 2
                else:
                    groups.append([k_])
                    k_ += 1
            if nkt > 1:
                groups.append([nkt - 1])
            masked = {0, nkt - 1}
            its = [(bi, q0, qn_, g) for bi, (q0, qn_) in enumerate(qblocks) for g in groups]
            LOOK = 1

            def emit_s(i):
                bi, q0, qn_, g = its[i]
                k2 = i % 2
                pt_ = PT2[i % 3]
                bufs = [PS[2 * k2].b, PS[2 * k2 + 1].b]
                for j_, kt in enumerate(g):
                    sb_ = PS[2 * k2 + j_]
                    mm(sb_.h[:, 0:qn_], KT.h[0:96, kt * 128:(kt + 1) * 128], Q.h[0:96, q0:q0 + qn_], True, True,
                       r=[KT.bs[kt], Q.b], w=[sb_.b])
                if len(g) == 1:
                    kt = g[0]
                    if kt in masked:
                        act(pt_.h[:, 0, 0:qn_], PS[2 * k2].h[:, 0:qn_], AF.Exp, r=[PS[2 * k2].b, kbs.b], w=[pt_.b],
                            bias=kbs.h[:, kt:kt + 1], scale=SCALE_A)
                    else:
                        act(pt_.h[:, 0, 0:qn_], PS[2 * k2].h[:, 0:qn_], AF.Exp, r=[PS[2 * k2].b], w=[pt_.b],
                            scale=SCALE_A)
                else:
                    src = P2[k2][:, :].rearrange("p (b c) -> p b c", b=2)[:, :, 0:qn_]
                    act(pt_.h[:, :, 0:qn_], src, AF.Exp, r=bufs, w=[pt_.b], scale=SCALE_A)

            def emit_pv(i):
                bi, q0, qn_, g = its[i]
                nsub = qn_ // 128
                acc = PS[4 + (bi % 2)]
                pt_ = PT2[i % 3]
                for j_, kt in enumerate(g):
                    for s in range(nsub):
                        mm(acc.h[:, s * 65:(s + 1) * 65], pt_.h[:, j_, s * 128:(s + 1) * 128], Vh.h[:, kt, 0:65],
                           (kt == 0 and s == 0), kt == nkt - 1, r=[pt_.b, Vh.b], w=[acc.b], skip=True)
                if g[-1] == nkt - 1:
                    r_ = rd[bi % 2]
                    a3 = acc.h[:, 0:nsub * 65].rearrange("p (s c) -> p s c", c=65)
                    recip(r_.h[:, 0:nsub], a3[:, :, 64], r=[acc.b], w=[r_.b])
                    for s in range(nsub):
                        ti = q0 // 128 + s
                        ts(attok.h[:, ti, (h % 2) * 64:(h % 2) * 64 + 64], a3[:, s, 0:64], r_.h[:, s:s + 1], None,
                           ALU.mult, None, r=[acc.b, r_.b], w=[attok.bs[ti]])

            for i in range(min(LOOK, len(its))):
                emit_s(i)
            for i in range(len(its)):
                if i + LOOK < len(its):
                    emit_s(i + LOOK)
                emit_pv(i)
            if h % 2 == 1:
                for (i0, in_) in blocks_of(NT, 4):
                    for i in range(i0, i0 + in_):
                        tr(psb(0, (i - i0) * 128, 128), attok.h[:, i, :], ident[:], r=[attok.bs[i], ident.b], w=[PB[0].b])
                    cp(attnT.h[:, h // 2, i0 * 128:(i0 + in_) * 128], psb(0, 0, in_ * 128), r=[PB[0].b], w=[attnT.b])
        S.barrier()
        A.release(m)

    def alloc_p3a_weights():
        return (A.alloc([128, 8, 2048], BF16), A.alloc([128, 4, D], BF16), A.alloc([128, 4, D], BF16),
                A.alloc([128, 8, D], BF16))

    def load_p3a_weights(ws):
        wg, woa, wom, wo = ws
        S.dma("sp", wg[:], s_win[:, :, 2480:4528], r=[bscr], w=[wg.b])
        S.dma("sp", woa[:], s_woa, r=[bscr], w=[woa.b])
        S.dma("sp", wom[:], s_wom, r=[bscr], w=[wom.b])
        S.dma("sp", wo[:], s_wout, r=[bscr], w=[wo.b])

    def phase3a(sl, pre=None):
        m = A.mark()
        if pre is None:
            pre = alloc_p3a_weights()
            load_p3a_weights(pre)
        wg, woa, wom, wo = pre
        ubf = [A.alloc([128, D], BF16) for _ in range(2)]
        junk = A.alloc([128, D], BF16)
        ssb = [A.alloc([128, 2], F32) for _ in range(2)]
        rsb = [A.alloc([128, 2], F32) for _ in range(2)]
        ut = [A.alloc([128, 8, 128], BF16) for _ in range(2)]
        sg = [A.alloc([128, 512], F32) for _ in range(2)]
        yT = A.alloc([128, 8, 128], BF16)
        h2 = [A.alloc([128, D], F32) for _ in range(2)]
        v128 = A.alloc([128, NT], F32)
        S.dma("sp", v128[:], valid128[sl], w=[v128.b])
        xb3 = [A.alloc([128, D], F32) for _ in range(3)]
        ybf2 = [A.alloc([128, D], BF16) for _ in range(2)]
        ubf2 = [A.alloc([128, D], BF16) for _ in range(2)]
        ss2 = [A.alloc([128, 2], F32) for _ in range(2)]
        rs2 = [A.alloc([128, 2], F32) for _ in range(2)]

        def sA(i):
            p = i % 2
            tsl = slice(i * 128, (i + 1) * 128)
            front_tile(xs[sl, tsl, :], xb3[i % 3], ubf[p], ssb[p], rsb[p], junk, 0, ut[p][:], [ut[p].b])

        def sB(i):
            p = i % 2
            tsl = slice(i * 128, (i + 1) * 128)
            u = ut[p]
            yb = ybf2[p]
            for half in range(2):
                c0 = half * 512
                for kt in range(8):
                    mm(PS[1].h[:, :], u.h[:, kt, :], wg.h[:, kt, c0:c0 + 512], kt == 0, kt == 7, r=[u.b, wg.b], w=[PS[1].b])
                for kt in range(4):
                    mm(PS[2].h[:, :], attnT.h[:, kt, tsl], woa.h[:, kt, c0:c0 + 512], kt == 0, kt == 3,
                       r=[attnT.b, woa.b], w=[PS[2].b])
                for kt in range(8):
                    mm(PS[3].h[:, :], u.h[:, kt, :], wg.h[:, kt, 1024 + c0:1024 + c0 + 512], kt == 0, kt == 7,
                       r=[u.b, wg.b], w=[PS[3].b])
                for kt in range(4):
                    mm(PS[4].h[:, :], hmT.h[:, kt, tsl], wom.h[:, kt, c0:c0 + 512], kt == 0, kt == 3,
                       r=[hmT.b, wom.b], w=[PS[4].b])
                act(sg[0][:], PS[1].h[:, :], AF.Sigmoid, r=[PS[1].b], w=[sg[0].b])
                act(sg[1][:], PS[3].h[:, :], AF.Sigmoid, r=[PS[3].b], w=[sg[1].b])
                tt(sg[0][:], sg[0][:], PS[2].h[:, :], ALU.mult, r=[sg[0].b, PS[2].b], w=[sg[0].b])
                tt(sg[1][:], sg[1][:], PS[4].h[:, :], ALU.mult, r=[sg[1].b, PS[4].b], w=[sg[1].b])
                tt(yb.h[:, c0:c0 + 512], sg[0][:], sg[1][:], ALU.add, r=[sg[0].b, sg[1].b], w=[yb.b])

        def sC(i):
            p = i % 2
            tsl = slice(i * 128, (i + 1) * 128)
            yb = ybf2[p]
            x_ = xb3[i % 3]
            for kt in range(8):
                tr(psb(1, kt * 128, 128), yb.h[:, kt * 128:(kt + 1) * 128], ident[:], r=[yb.b, ident.b], w=[PB[1].b])
            cp(yT[:], psb(1, 0, 1024).rearrange("p (k t) -> p k t", k=8), r=[PB[1].b], w=[yT.b], eng="act")
            hh_ = h2[p]
            for half in range(2):
                c0 = half * 512
                pb = PS[5] if half == 0 else PS[0]
                for kt in range(8):
                    mm(pb.h[:, :], yT.h[:, kt, :], wo.h[:, kt, c0:c0 + 512], kt == 0, kt == 7, r=[yT.b, wo.b], w=[pb.b])
                tt(hh_.h[:, c0:c0 + 512], pb.h[:, :], x_.h[:, c0:c0 + 512], ALU.add, r=[pb.b, x_.b], w=[hh_.b])
            S.dma("pool", s_h2[tsl, :], hh_[:], r=[hh_.b], w=[bh2[i]])
            act(junk[:], hh_[:], AF.Square, r=[hh_.b], w=[junk.b, ss2[p].b], accum=ss2[p].h[:, 0:1])
            rstd_from_ss(rs2[p].h[:, 0:1], ss2[p].h[:, 0:1], D, r=[ss2[p].b], w=[rs2[p].b])
            tt(rs2[p].h[:, 0:1], rs2[p].h[:, 0:1], v128.h[:, i:i + 1], ALU.mult, r=[rs2[p].b, v128.b], w=[rs2[p].b])
            ts(ubf2[p][:], hh_[:], rs2[p].h[:, 0:1], None, ALU.mult, None, r=[hh_.b, rs2[p].b], w=[ubf2[p].b])

        def sD(i):
            p = i % 2
            tsl = slice(i * 128, (i + 1) * 128)
            for kt in range(8):
                tr(psb(0, kt * 128, 128), ubf2[p].h[:, kt * 128:(kt + 1) * 128], ident[:], r=[ubf2[p].b, ident.b], w=[PB[0].b])
            cp(uT.h[:, :, tsl], psb(0, 0, 1024).rearrange("p (k t) -> p k t", k=8), r=[PB[0].b], w=[uT.bs[i]], eng="act")

        for it in range(NT + 3):
            if 0 <= it - 3 < NT:
                sD(it - 3)
            if 0 <= it - 2 < NT:
                sC(it - 2)
            if 0 <= it - 1 < NT:
                sB(it - 1)
            if it < NT:
                sA(it)
        S.barrier()
        A.release(m)

    def phase3b(sl):
        m = A.mark()
        G = 2
        NG = NFT // G
        wu = [A.alloc([128, G, 8, 128], BF16) for _ in range(2)]
        wgt = [A.alloc([128, G, 8, 128], BF16) for _ in range(2)]
        hid = A.alloc([128, NFT, 512], BF16, nb=NFT)
        ab = [A.alloc([128, 516], F32) for _ in range(2)]
        cb_ = [A.alloc([128, 512], F32) for _ in range(2)]
        gb_ = [A.alloc([128, 512], F32) for _ in range(2)]
        h2t = [A.alloc([128, D], F32) for _ in range(4)]
        ot = [A.alloc([128, D], F32) for _ in range(2)]
        junk = A.alloc([128, D], BF16)
        ss = [A.alloc([128, 2], F32) for _ in range(2)]
        octr = 0
        for (b0, bn) in blocks_of(NOWNT, 512):
            t0 = 64 + b0
            tl = sorted(set([(t0 - 1) // 128] + list(range(t0 // 128, (t0 + bn) // 128 + 1))))
            tl = [i for i in tl if i < NT]
            nsubs = (bn + 127) // 128
            for s in range(nsubs):
                r0 = t0 + s * 128
                rn = min(128, bn - s * 128)
                rtl = sorted(set([r0 // 128, (r0 + rn - 1) // 128]))
                S.dma("sp", h2t[s].h[0:rn, :], s_h2[r0:r0 + rn, :], r=[bh2[i] for i in rtl], w=[h2t[s].b])
            for g in range(NG):
                p = g % 2
                S.dma("sp", wu[p][:], s_wup[g * G:(g + 1) * G].rearrange("f p k c -> p f k c"), r=[bscr], w=[wu[p].b])
                S.dma("sp", wgt[p][:], s_wgate[g * G:(g + 1) * G].rearrange("f p k c -> p f k c"), r=[bscr], w=[wgt[p].b])
                for fi in range(G):
                    f = g * G + fi
                    q = f % 2
                    pa = PS[0 + q]
                    pa2 = PS[2 + q]
                    pg = PS[4 + q]
                    ub = [uT.bs[i] for i in tl]
                    for kt in range(8):
                        mm(pa.h[:, 0:bn], wu[p].h[:, fi, kt, :], uT.h[:, kt, t0 - 1:t0 - 1 + bn], kt == 0, kt == 7,
                           r=[wu[p].b] + ub, w=[pa.b])
                    for kt in range(8):
                        mm(pa2.h[:, 0:2], wu[p].h[:, fi, kt, :], uT.h[:, kt, t0 - 1 + bn:t0 + 1 + bn], kt == 0, kt == 7,
                           r=[wu[p].b] + ub, w=[pa2.b])
                    for kt in range(8):
                        mm(pg.h[:, 0:bn], wgt[p].h[:, fi, kt, :], uT.h[:, kt, t0:t0 + bn], kt == 0, kt == 7,
                           r=[wgt[p].b] + ub, w=[pg.b])
                    a_ = ab[q]
                    cp(a_.h[:, 0:bn], pa.h[:, 0:bn], r=[pa.b], w=[a_.b], eng="act")
                    cp(a_.h[:, bn:bn + 2], pa2.h[:, 0:2], r=[pa2.b], w=[a_.b], eng="act")
                    c_ = cb_[q]
                    ts(c_.h[:, 0:bn], a_.h[:, 0:bn], cvp.h[:, f, 0:1], None, ALU.mult, None, r=[a_.b, cvp.b], w=[c_.b])
                    stt(c_.h[:, 0:bn], a_.h[:, 1:bn + 1], cvp.h[:, f, 1:2], c_.h[:, 0:bn], ALU.mult, ALU.add,
                        r=[a_.b, cvp.b, c_.b], w=[c_.b])
                    stt(c_.h[:, 0:bn], a_.h[:, 2:bn + 2], cvp.h[:, f, 2:3], c_.h[:, 0:bn], ALU.mult, ALU.add,
                        r=[a_.b, cvp.b, c_.b], w=[c_.b])
                    g_ = gb_[q]
                    act(g_.h[:, 0:bn], c_.h[:, 0:bn], AF.Gelu_apprx_tanh, r=[c_.b, cvp.b], w=[g_.b], bias=cvp.h[:, f, 3:4])
                    tt(hid.h[:, f, 0:bn], g_.h[:, 0:bn], pg.h[:, 0:bn], ALU.mult, r=[g_.b, pg.b], w=[hid.bs[f]])
            for s in range(nsubs):
                rn = min(128, bn - s * 128)
                o_ = ot[octr % 2]
                sq = ss[octr % 2]
                octr += 1
                for half in range(2):
                    c0 = half * 512
                    pb = PS[half]
                    for f in range(NFT):
                        mm(pb.h[0:rn, :], hid.h[:, f, s * 128:s * 128 + rn], wdfull.h[:, f, c0:c0 + 512], f == 0, f == NFT - 1,
                           r=[hid.bs[f], wdfull.b], w=[pb.b])
                    tt(o_.h[0:rn, c0:c0 + 512], pb.h[0:rn, :], h2t[s].h[0:rn, c0:c0 + 512], ALU.add,
                       r=[pb.b, h2t[s].b], w=[o_.b])
                act(junk.h[0:rn, :], o_.h[0:rn, :], AF.Square, r=[o_.b], w=[junk.b, sq.b], accum=sq.h[0:rn, 0:1])
                rstd_from_ss(sq.h[0:rn, 1:2], sq.h[0:rn, 0:1], D, r=[sq.b], w=[sq.b])
                stt(o_.h[0:rn, :], o_.h[0:rn, :], sq.h[0:rn, 1:2], gfin.h[0:rn, :], ALU.mult, ALU.mult,
                    r=[o_.b, sq.b, gfin.b], w=[o_.b])
                S.dma("pool", y[sl, b0 + s * 128:b0 + s * 128 + rn, :], o_.h[0:rn, :], r=[o_.b], w=[])
        S.barrier()
        A.release(m)

    wdfull = None

    def phase3b_wrapped(sl):
        nonlocal wdfull
        m = A.mark()
        wdfull = A.alloc([128, NFT, D], BF16)
        S.dma("sp", wdfull[:], s_wdown, r=[bscr], w=[wdfull.b])
        phase3b(sl)
        A.release(m)

    import os
    STOP = int(os.environ.get("KSTOP", "99"))
    for sl in range(NSL):
        if STOP == 0:
            break
        sample = (sl == NPS)
        nkt = NKF if sample else NT
        A.release(m_slice)
        qnT = A.alloc([128, 2, L], BF16, nb=NT)
        hmT = A.alloc([128, 4, L], BF16, nb=1)
        KTp = cnTp = None
        if sample:
            scan_pass()
        else:
            KTp = A.alloc([96, NT * 128], BF16, nb=NT)
            cnTp = A.alloc([128, NT * 128], BF16, nb=NT)
        phase0(sl, None if sample else (KTp, cnTp))
        if STOP == 1:
            break
        phase1(sl, sample)
        if STOP == 2:
            break
        attnT = A.alloc([128, 4, L], BF16, nb=1)
        pre3 = None if sample else alloc_p3a_weights()
        m = A.mark()
        A.relaxed = True
        if sample:
            KT = A.alloc([96, nkt * 128], BF16, nb=nkt)
            cnT = A.alloc([128, nkt * 128], BF16, nb=nkt)
            kv_pass(lambda j: xfull[j * 128:(j + 1) * 128, :], nkt, cs_full, KT, cnT)
            phase2(sl, nkt, KT, cnT, kbias_full)
        else:
            phase2(sl, nkt, KTp, cnTp, kbias_sl[sl], prefetch=lambda: load_p3a_weights(pre3))
        A.release(m)
        A.relaxed = False
        if STOP == 3:
            break
        phase3a(sl, pre3)
        if STOP == 4:
            break
        A.release(m_slice)
        phase3b_wrapped(sl)
        if STOP == 5:
            break
    S.barrier(engines=("sp",))
    S.emit()
    return nc, S


def _rope_tables(npos):
    inv = (10000.0 ** (-np.arange(0, 32, 2, dtype=np.float32) / np.float32(32))).astype(np.float32)
    ang = np.arange(npos, dtype=np.float32)[:, None] * inv[None, :]
    return np.cos(ang).astype(np.float32), np.sin(ang).astype(np.float32)


def make_inputs(NPS, NOWN, NKF, S_full, x_prompt, x_sample, meta_tokens, norm_mix, w_in, q_norm, w_uq, kv_norm,
                w_ukv, b_igate, b_fgate, mlstm_norm, w_o_attn, w_o_mlstm, w_out, norm_ffn, w_up, w_gate,
                conv_w, conv_b, w_down, norm_final):
    NSL = NPS + 1
    NCH = NOWN + 2
    L = 64 * NCH
    NT = L // 128
    f32 = np.float32
    meta = np.asarray(meta_tokens, f32)
    SEQ = x_prompt.shape[1]
    assert SEQ == 64 * NOWN and S_full == 8 * 64 * NOWN
    TF = 64 + S_full
    PF = NKF * 128
    cos_all, sin_all = _rope_tables(TF)

    def cs_for_positions(pos):
        rp = np.clip(pos - 48, 0, TF - 1)
        return np.concatenate([cos_all[rp], sin_all[rp]], axis=1).astype(f32)

    xfull = np.zeros((PF, D), f32)
    xfull[48:64] = meta
    xfull[64:64 + S_full] = np.asarray(x_sample, f32)[0]
    posfull = np.arange(PF)
    validfull_v = ((posfull >= 48) & (posfull < TF)).astype(f32)
    validfull = np.ascontiguousarray(validfull_v.reshape(NKF, 128).T)
    kbias_full = np.ascontiguousarray(((1.0 - validfull_v) * KBIAS_NEG).astype(f32).reshape(NKF, 128).T)
    cs_full = np.ascontiguousarray(cs_for_positions(posfull).reshape(NKF, 128, 32).transpose(1, 0, 2))

    def rep(v, n=128):
        return np.ascontiguousarray(np.broadcast_to(np.asarray(v, f32).reshape(1, -1), (n, np.asarray(v).size)))

    def pk(v, kt):
        return np.ascontiguousarray(np.asarray(v, f32).reshape(kt, 128).T)

    cmat = np.zeros((128, 5, 128), f32)
    s_idx = np.arange(128)[:, None]
    t_idx = np.arange(128)[None, :]
    cmat[:64, 0, :64] = (s_idx[:64, :] <= t_idx[:, :64])
    cmat[:64, 1, :64] = (s_idx[:64, :] >= t_idx[:, :64])
    cmat[:64, 2, :64] = (s_idx[:64, :] <= t_idx[:, :64])
    cmat[:64, 3, :64] = (s_idx[:64, :] >= t_idx[:, :64])
    cmat[:, 4, :] = (s_idx <= t_idx)
    convp = np.zeros((128, NFT, 4), f32)
    cw = np.asarray(conv_w, f32)[0]
    cbias = np.asarray(conv_b, f32)[0]
    for j in range(3):
        convp[:, :, j] = cw[j].reshape(NFT, 128).T
    convp[:, :, 3] = cbias.reshape(NFT, 128).T
    bgate = rep(np.concatenate([np.asarray(b_igate, f32).reshape(-1), np.asarray(b_fgate, f32).reshape(-1)]))
    common = dict(
        xfull=xfull, w_in=np.ascontiguousarray(np.asarray(w_in, f32)[0]), w_uq=np.ascontiguousarray(np.asarray(w_uq, f32)[0]),
        w_ukv=np.ascontiguousarray(np.asarray(w_ukv, f32)[0]), w_oa=np.ascontiguousarray(np.asarray(w_o_attn, f32)[0]),
        w_om=np.ascontiguousarray(np.asarray(w_o_mlstm, f32)[0]), w_out=np.ascontiguousarray(np.asarray(w_out, f32)[0]),
        w_up=np.ascontiguousarray(np.asarray(w_up, f32)[0]), w_gate=np.ascontiguousarray(np.asarray(w_gate, f32)[0]),
        w_down=np.ascontiguousarray(np.asarray(w_down, f32)[0]),
        g_mix=pk(np.asarray(norm_mix)[0], 8), g_q=pk(np.asarray(q_norm)[0], 2), g_kv=pk(np.asarray(kv_norm)[0], 1),
        g_ml=pk(np.asarray(mlstm_norm)[0], 4), g_ffn=pk(np.asarray(norm_ffn)[0], 8), g_fin=rep(norm_final),
        bgate=bgate, convp=convp, kbias_full=kbias_full, validfull=validfull, cs_full=cs_full, cmat=cmat,
        identsrc=np.eye(128, dtype=f32),
    )
    xp = np.asarray(x_prompt, f32)
    in_maps = []
    for c in range(NCORES):
        xs = np.zeros((NSL, L, D), f32)
        valid = np.zeros((NSL, L), f32)
        pos = np.zeros((NSL, L), np.int64)
        for i in range(NPS):
            xs[i, 48:64] = meta
            xs[i, 64:64 + SEQ] = xp[c * NPS + i]
            valid[i, 48:64 + SEQ] = 1.0
            pos[i] = np.arange(L)
        g0 = 64 * NOWN * c
        gp = g0 + np.arange(L)
        ok = gp < PF
        xs[NPS, ok] = xfull[gp[ok]]
        valid[NPS] = ((gp >= 48) & (gp < TF)).astype(f32)
        pos[NPS] = gp
        cs = np.stack([cs_for_positions(pos[i]) for i in range(NSL)])
        cs_tok = np.ascontiguousarray(cs.reshape(NSL, NT, 128, 32).transpose(0, 2, 1, 3))
        csT = np.zeros((NSL, 2, 32, L), f32)
        csT[:, 0, 0:16] = cs[:, :, 0:16].transpose(0, 2, 1)
        csT[:, 0, 16:32] = cs[:, :, 0:16].transpose(0, 2, 1)
        csT[:, 1, 0:16] = cs[:, :, 16:32].transpose(0, 2, 1)
        csT[:, 1, 16:32] = cs[:, :, 16:32].transpose(0, 2, 1)
        valid128 = np.ascontiguousarray(valid.reshape(NSL, NT, 128).transpose(0, 2, 1))
        valid64 = np.ascontiguousarray(valid.reshape(NSL, NCH, 64).transpose(0, 2, 1))
        kbias_sl = np.ascontiguousarray(((1.0 - valid[:NPS]) * KBIAS_NEG).astype(f32).reshape(NPS, NT, 128).transpose(0, 2, 1))
        tj = np.arange(NKF)
        jl = (64 * NOWN * c) // 128
        actf = (tj < jl).astype(f32)
        actb = (tj >= jl + NT).astype(f32)
        a8 = np.zeros((128, NKF, 8), f32)
        a8[:, :, 0:4] = actf[None, :, None]
        a8[:, :, 4:8] = actb[None, :, None]
        d = dict(common)
        d.update(xs=xs, valid128=valid128, valid64=valid64, kbias_sl=kbias_sl, act8=a8, cs_tok=cs_tok, csT=csT)
        in_maps.append(d)
    return in_maps


_CACHE = {}


def run(NPS, NOWN, NKF, S_full, inputs, trace=False):
    key = (NPS, NOWN, NKF)
    if key not in _CACHE:
        _CACHE[key] = build_program(NPS, NOWN, NKF)
    nc, S = _CACHE[key]
    in_maps = make_inputs(NPS, NOWN, NKF, S_full, **inputs)
    res = run_bass_kernel_spmd(nc, in_maps, core_ids=list(range(NCORES)))
    ys = [np.asarray(r["y"]) for r in res.results]
    NOWNT = 64 * NOWN
    y_prompt = np.concatenate([ys[c][:NPS] for c in range(NCORES)], axis=0).astype(np.float32)
    y_sample = np.concatenate([ys[c][NPS] for c in range(NCORES)], axis=0)[None].astype(np.float32)
    return y_prompt, y_sample


def kernel(**inputs):
    x_prompt = np.asarray(inputs["x_prompt"])
    x_sample = np.asarray(inputs["x_sample"])
    B, SEQ, _ = x_prompt.shape
    S_full = x_sample.shape[1]
    NPS = B // NCORES
    NOWN = SEQ // 64
    NKF = (64 + S_full + 127) // 128
    return run(NPS, NOWN, NKF, S_full, inputs)
```
